# Optimizing a Trainium2 kernel written in Bass

```python
import math
import jax
import jax.numpy as jnp
from jax import lax
import numpy as np

D_MODEL = 1024
BATCH = 2
SEQ = 8192
DEPTH = 1

PLE_DIM = 256
EPS = 1e-6
SSM_GROUP = 16
SSM_STATE = 64
SSM_WIDTH = D_MODEL // 2
SSM_GROUPS = SSM_WIDTH // SSM_GROUP
HEAD_DIM = 64
DILATED_PATTERNS = ((128, 1), (512, 4), (2048, 16))
N_DIL = len(DILATED_PATTERNS)
HEADS_PER_GROUP = 4
N_ATTN_HEADS = N_DIL * HEADS_PER_GROUP
ATTN_WIDTH = N_ATTN_HEADS * HEAD_DIM
ATTN_OUT = HEADS_PER_GROUP * HEAD_DIM
ROT_DIM = HEAD_DIM // 4
ROPE_THETA = 500000.0
Q_BLOCK = 128
D_FF = 2816
CONV_WIDTH = 3
IN_WIDTH = SSM_WIDTH + 3 * ATTN_WIDTH + 2 * D_MODEL
NEG_BIG = -1e30

kernel_name = 'hybrid_s5_dilated_attn_block'


def rms_norm(x, g):
    xf = x.astype(jnp.float32)
    var = jnp.mean(xf * xf, axis=-1, keepdims=True)
    return (xf * lax.rsqrt(var + EPS) * g.astype(jnp.float32)).astype(x.dtype)


def _complex_linear_combine(e1, e2):
    a1r, a1i, b1r, b1i = e1
    a2r, a2i, b2r, b2i = e2
    ar = a2r * a1r - a2i * a1i
    ai = a2r * a1i + a2i * a1r
    br = a2r * b1r - a2i * b1i + b2r
    bi = a2r * b1i + a2i * b1r + b2i
    return (ar, ai, br, bi)


def s5_mixer(u, lam_re, lam_im, log_dt, b_re, b_im, c_re, c_im, d_skip, glu_w, glu_b):
    bsz, l = u.shape[0], u.shape[1]
    uf = u.astype(jnp.float32).reshape(bsz, l, SSM_GROUPS, SSM_GROUP)
    lr = lam_re.astype(jnp.float32)
    li = lam_im.astype(jnp.float32)
    dt = jnp.exp(log_dt.astype(jnp.float32))[:, None]
    mag = jnp.exp(lr * dt)
    ar = mag * jnp.cos(li * dt)
    ai = mag * jnp.sin(li * dt)
    den = lr * lr + li * li
    cr = ((ar - 1.0) * lr + ai * li) / den
    ci = (ai * lr - (ar - 1.0) * li) / den
    br_ = b_re.astype(jnp.float32)
    bi_ = b_im.astype(jnp.float32)
    bbar_re = cr[..., None] * br_ - ci[..., None] * bi_
    bbar_im = cr[..., None] * bi_ + ci[..., None] * br_
    bu_re = jnp.einsum('blgh,gph->blgp', uf, bbar_re)
    bu_im = jnp.einsum('blgh,gph->blgp', uf, bbar_im)
    a_re = jnp.broadcast_to(ar, bu_re.shape)
    a_im = jnp.broadcast_to(ai, bu_re.shape)
    _, _, s_re, s_im = lax.associative_scan(_complex_linear_combine, (a_re, a_im, bu_re, bu_im), axis=1)
    y = (jnp.einsum('blgp,ghp->blgh', s_re, c_re.astype(jnp.float32))
         - jnp.einsum('blgp,ghp->blgh', s_im, c_im.astype(jnp.float32))
         + d_skip.astype(jnp.float32) * uf)
    y = jax.nn.gelu(y.reshape(bsz, l, SSM_WIDTH))
    y = y * jax.nn.sigmoid(y @ glu_w.astype(jnp.float32) + glu_b.astype(jnp.float32))
    return y.astype(u.dtype)


def rotary(x, pos):
    half = ROT_DIM // 2
    freqs = ROPE_THETA ** (-jnp.arange(half, dtype=jnp.float32) * (2.0 / ROT_DIM))
    ang = pos[:, None] * freqs[None, :]
    cos = jnp.cos(ang)[None, :, None, :]
    sin = jnp.sin(ang)[None, :, None, :]
    xf = x.astype(jnp.float32)
    x1 = xf[..., :half]
    x2 = xf[..., half:ROT_DIM]
    out = jnp.concatenate([x1 * cos - x2 * sin, x2 * cos + x1 * sin, xf[..., ROT_DIM:]], axis=-1)
    return out.astype(x.dtype)


def dilated_attention(q, k, v):
    bsz, l = q.shape[0], q.shape[1]
    pos = jnp.arange(l, dtype=jnp.float32)
    q = rotary(q, pos)
    k = rotary(k, pos)
    qs = [q[:, :, g * HEADS_PER_GROUP:(g + 1) * HEADS_PER_GROUP] for g in range(N_DIL)]
    ks = [k[:, :, g * HEADS_PER_GROUP:(g + 1) * HEADS_PER_GROUP] for g in range(N_DIL)]
    vs = [v[:, :, g * HEADS_PER_GROUP:(g + 1) * HEADS_PER_GROUP] for g in range(N_DIL)]
    scale = HEAD_DIM ** -0.5

    def block(bi):
        start = bi * Q_BLOCK
        t = start + jnp.arange(Q_BLOCK, dtype=jnp.int32)
        ms, dens, nums = [], [], []
        for g, (window, dil) in enumerate(DILATED_PATTERNS):
            n_keys = window // dil + 1
            idx = t[:, None] - dil * jnp.arange(n_keys, dtype=jnp.int32)[None, :]
            valid = idx >= 0
            idx = jnp.maximum(idx, 0)
            qb = lax.dynamic_slice_in_dim(qs[g], start, Q_BLOCK, axis=1)
            kb = jnp.take(ks[g], idx, axis=1)
            vb = jnp.take(vs[g], idx, axis=1)
            s = jnp.einsum('bqhd,bqjhd->bqhj', qb, kb).astype(jnp.float32) * scale
            s = jnp.where(valid[None, :, None, :], s, NEG_BIG)
            m = jnp.max(s, axis=-1)
            e = jnp.exp(s - m[..., None])
            dens.append(jnp.sum(e, axis=-1))
            nums.append(jnp.einsum('bqhj,bqjhd->bqhd', e, vb.astype(jnp.float32)))
            ms.append(m)
        m_all = jnp.stack(ms)
        w = jnp.exp(m_all - jnp.max(m_all, axis=0))
        num = sum(w[g][..., None] * nums[g] for g in range(N_DIL))
        den = sum(w[g] * dens[g] for g in range(N_DIL))
        return (num / den[..., None]).astype(q.dtype)

    out = lax.map(block, jnp.arange(l // Q_BLOCK, dtype=jnp.int32))
    return out.transpose(1, 0, 2, 3, 4).reshape(bsz, l, ATTN_OUT)


def causal_depthwise_conv(x, w, bias):
    rhs = w.reshape(CONV_WIDTH, 1, w.shape[-1])
    y = lax.conv_general_dilated(x, rhs, window_strides=(1,), padding=((CONV_WIDTH - 1, 0),),
                                 dimension_numbers=('NWC', 'WIO', 'NWC'), feature_group_count=x.shape[-1])
    return y + bias


def setup_inputs(seed: int = 0) -> dict:
    key = jax.random.key(seed)
    ks = jax.random.split(key, 32)
    f32 = jnp.float32

    def nrm(k, shape, fan_in):
        return jax.random.normal(k, shape, f32) * (fan_in ** -0.5)

    def gain(k, shape):
        return 1.0 + 0.02 * jax.random.normal(k, shape, f32)

    lam_im = jnp.broadcast_to(math.pi * jnp.arange(SSM_STATE, dtype=f32), (DEPTH, SSM_GROUPS, SSM_STATE))
    return {
        'x': jax.random.normal(ks[0], (BATCH, SEQ, D_MODEL), f32),
        'p': jax.random.normal(ks[1], (DEPTH, BATCH, SEQ, PLE_DIM), f32),
        'mix_norm_g': gain(ks[2], (DEPTH, D_MODEL)),
        'w_in': nrm(ks[3], (DEPTH, D_MODEL, IN_WIDTH), D_MODEL),
        'gate_b': 0.02 * jax.random.normal(ks[4], (DEPTH, 2 * D_MODEL), f32),
        'ssm_lam_re': -0.5 + 0.01 * jax.random.normal(ks[5], (DEPTH, SSM_GROUPS, SSM_STATE), f32),
        'ssm_lam_im': lam_im + 0.01 * jax.random.normal(ks[6], (DEPTH, SSM_GROUPS, SSM_STATE), f32),
        'ssm_log_dt': jax.random.uniform(ks[7], (DEPTH, SSM_GROUPS), f32, minval=math.log(1e-3), maxval=math.log(1e-1)),
        'ssm_b_re': nrm(ks[8], (DEPTH, SSM_GROUPS, SSM_STATE, SSM_GROUP), 2 * SSM_GROUP),
        'ssm_b_im': nrm(ks[9], (DEPTH, SSM_GROUPS, SSM_STATE, SSM_GROUP), 2 * SSM_GROUP),
        'ssm_c_re': nrm(ks[10], (DEPTH, SSM_GROUPS, SSM_GROUP, SSM_STATE), 2 * SSM_STATE),
        'ssm_c_im': nrm(ks[11], (DEPTH, SSM_GROUPS, SSM_GROUP, SSM_STATE), 2 * SSM_STATE),
        'ssm_d': jax.random.normal(ks[12], (DEPTH, SSM_GROUPS, SSM_GROUP), f32),
        'ssm_glu_w': nrm(ks[13], (DEPTH, SSM_WIDTH, SSM_WIDTH), SSM_WIDTH),
        'ssm_glu_b': 0.02 * jax.random.normal(ks[14], (DEPTH, SSM_WIDTH), f32),
        'w_branch_a': nrm(ks[15], (DEPTH, SSM_WIDTH, D_MODEL), SSM_WIDTH),
        'w_branch_b': nrm(ks[16], (DEPTH, ATTN_OUT, D_MODEL), ATTN_OUT),
        'w_out': nrm(ks[17], (DEPTH, D_MODEL, D_MODEL), D_MODEL),
        'ffn_norm_g': gain(ks[18], (DEPTH, D_MODEL)),
        'ffn_w_gate': nrm(ks[19], (DEPTH, D_MODEL, D_FF), D_MODEL),
        'ffn_w_up': nrm(ks[20], (DEPTH, D_MODEL, D_FF), D_MODEL),
        'ffn_conv_w': nrm(ks[21], (DEPTH, CONV_WIDTH, D_FF), CONV_WIDTH),
        'ffn_conv_b': 0.02 * jax.random.normal(ks[22], (DEPTH, D_FF), f32),
        'ffn_w_down': nrm(ks[23], (DEPTH, D_FF, D_MODEL), D_FF),
        'ple_norm_g': gain(ks[24], (DEPTH, D_MODEL)),
        'ple_w_gate': nrm(ks[25], (DEPTH, D_MODEL, D_MODEL), D_MODEL),
        'ple_w_proj': nrm(ks[26], (DEPTH, PLE_DIM, D_MODEL), PLE_DIM),
        'final_norm_g': gain(ks[27], (D_MODEL,)),
    }


def reference(x, p, mix_norm_g, w_in, gate_b, ssm_lam_re, ssm_lam_im, ssm_log_dt, ssm_b_re, ssm_b_im,
              ssm_c_re, ssm_c_im, ssm_d, ssm_glu_w, ssm_glu_b, w_branch_a, w_branch_b, w_out,
              ffn_norm_g, ffn_w_gate, ffn_w_up, ffn_conv_w, ffn_conv_b, ffn_w_down,
              ple_norm_g, ple_w_gate, ple_w_proj, final_norm_g):
    bsz, l = x.shape[0], x.shape[1]
    o_q = SSM_WIDTH
    o_k = o_q + ATTN_WIDTH
    o_v = o_k + ATTN_WIDTH
    o_ga = o_v + ATTN_WIDTH
    o_gb = o_ga + D_MODEL
    h = x
    for i in range(DEPTH):
        u = rms_norm(h, mix_norm_g[i])
        z = u @ w_in[i]
        s_in = z[..., :o_q]
        q = z[..., o_q:o_k].reshape(bsz, l, N_ATTN_HEADS, HEAD_DIM)
        k = z[..., o_k:o_v].reshape(bsz, l, N_ATTN_HEADS, HEAD_DIM)
        v = z[..., o_v:o_ga].reshape(bsz, l, N_ATTN_HEADS, HEAD_DIM)
        g_a = jax.nn.sigmoid(z[..., o_ga:o_gb] + gate_b[i, :D_MODEL])
        g_b = jax.nn.sigmoid(z[..., o_gb:] + gate_b[i, D_MODEL:])
        y_a = s5_mixer(s_in, ssm_lam_re[i], ssm_lam_im[i], ssm_log_dt[i], ssm_b_re[i], ssm_b_im[i],
                       ssm_c_re[i], ssm_c_im[i], ssm_d[i], ssm_glu_w[i], ssm_glu_b[i]) @ w_branch_a[i]
        y_b = dilated_attention(q, k, v) @ w_branch_b[i]
        h = h + (g_a * y_a + g_b * y_b) @ w_out[i]
        u2 = rms_norm(h, ffn_norm_g[i])
        gate = causal_depthwise_conv(u2 @ ffn_w_gate[i], ffn_conv_w[i], ffn_conv_b[i])
        h = h + (jax.nn.gelu(gate) * (u2 @ ffn_w_up[i])) @ ffn_w_down[i]
        u3 = rms_norm(h, ple_norm_g[i])
        h = h + jax.nn.sigmoid(u3 @ ple_w_gate[i]) * (p[i] @ ple_w_proj[i])
    return rms_norm(h, final_norm_g)
```

```python
import math
from contextlib import ExitStack
import numpy as np
import ml_dtypes
import concourse.bass as bass
import concourse.mybir as mybir
from concourse.bass_utils import run_bass_kernel_spmd

F32 = mybir.dt.float32
BF16 = mybir.dt.bfloat16
I32 = mybir.dt.int32
ALU = mybir.AluOpType
AF = mybir.ActivationFunctionType

D = 1024
SEQ = 8192
OWN = 2048
OWN17 = 2176
EXT = 8320
NT_EXT = 65
DFF = 2816
NF = 22
PI = math.pi
TWO_PI = 2 * math.pi
DILS = (1, 4, 16)
KP = ([-1, -2, -3, -4, -5, -6, -7, -8] + list(range(1, 17))
      + [112, 96, 80, 64, 48, 32, 16, 0]
      + [128 * (15 - k) for k in range(16)]
      + [2048])
NKP = len(KP)
EPS = 1e-6


class Sched:
    ENGS = ['sync', 'scalar', 'vector', 'gpsimd', 'tensor']

    def __init__(self, nc, es, nds=8):
        self.nc = nc
        self.sem = {e: es.enter_context(nc.semaphore('s_' + e)) for e in self.ENGS}
        self.cnt = {e: 0 for e in self.ENGS}
        self.prog = {e: [] for e in self.ENGS}
        self.seen = {e: {} for e in self.ENGS}
        self.lastw = {}
        self.readers = {}
        self.nds = nds
        self.dsem = {}
        self.dval = {}
        self.dnext = {}
        for q in ('sync', 'scalar', 'gpsimd'):
            self.dsem[q] = [es.enter_context(nc.semaphore('d_%s%d' % (q, i))) for i in range(nds)]
            self.dval[q] = [0] * nds
            self.dnext[q] = 0

    def semh(self, sk):
        if isinstance(sk, str):
            return self.sem[sk]
        return self.dsem[sk[0]][sk[1]]

    def _waits(self, eng, reads, writes):
        need = {}

        def add(tok):
            sk, v = tok
            if need.get(sk, 0) < v:
                need[sk] = v
        for r in reads:
            if r in self.lastw:
                add(self.lastw[r])
        for w in writes:
            if w in self.lastw:
                add(self.lastw[w])
            for sk, v in self.readers.get(w, {}).items():
                add((sk, v))
        out = []
        for sk, v in need.items():
            if self.seen[eng].get(sk, 0) >= v:
                continue
            if sk == eng and eng == 'tensor':
                continue
            self.seen[eng][sk] = v
            out.append((sk, v))
        return out

    def _record(self, tok, reads, writes):
        for r in reads:
            d = self.readers.setdefault(r, {})
            if d.get(tok[0], 0) < tok[1]:
                d[tok[0]] = tok[1]
        for w in writes:
            self.lastw[w] = tok
            self.readers[w] = {}

    def op(self, eng, fn, reads=(), writes=()):
        waits = self._waits(eng, reads, writes)
        self.cnt[eng] += 1
        tok = (eng, self.cnt[eng])
        self.prog[eng].append((waits, fn, (self.sem[eng], 1)))
        self._record(tok, reads, writes)

    def dma(self, q, out, in_, reads=(), writes=(), **kw):
        waits = self._waits(q, reads, writes)
        i = self.dnext[q]
        self.dnext[q] = (i + 1) % self.nds
        sk = (q, i)
        if self.dval[q][i] > 0 and self.seen[q].get(sk, 0) < self.dval[q][i]:
            waits.append((sk, self.dval[q][i]))
            self.seen[q][sk] = self.dval[q][i]
        self.dval[q][i] += 16
        tok = (sk, self.dval[q][i])
        self.prog[q].append((waits, (lambda e: e.dma_start(out=out, in_=in_, **kw)), (self.dsem[q][i], 16)))
        self._record(tok, reads, writes)

    def barrier(self):
        for e in self.ENGS:
            waits = []
            for o in self.ENGS:
                if o != e and self.cnt[o] > 0 and self.seen[e].get(o, 0) < self.cnt[o]:
                    waits.append((o, self.cnt[o]))
                    self.seen[e][o] = self.cnt[o]
            for q in self.dsem:
                for i in range(self.nds):
                    sk = (q, i)
                    if self.dval[q][i] > 0 and self.seen[e].get(sk, 0) < self.dval[q][i]:
                        waits.append((sk, self.dval[q][i]))
                        self.seen[e][sk] = self.dval[q][i]
            if e != 'tensor' and self.cnt[e] > 0 and self.seen[e].get(e, 0) < self.cnt[e]:
                waits.append((e, self.cnt[e]))
                self.seen[e][e] = self.cnt[e]
            if waits:
                self.prog[e].append((waits, None, None))
        self.lastw = {}
        self.readers = {}

    def emit(self, block):
        for e in self.ENGS:
            def mk(e):
                def f(eng):
                    for waits, fn, inc in self.prog[e]:
                        for sk, v in waits:
                            eng.wait_ge(self.semh(sk), v)
                        if fn is not None:
                            ins = fn(eng)
                            ins.then_inc(inc[0], inc[1])
                return f
            getattr(block, e)(mk(e))


class Arena:
    def __init__(self, t, cap_bytes):
        self.t = t
        self.cap = cap_bytes
        self.off = 0
        self.top = cap_bytes
        self.peak = 0

    def mark(self):
        return self.off

    def release(self, m):
        self.off = m

    def mark_top(self):
        return self.top

    def release_top(self, m):
        self.top = m

    def _view(self, off, nb, shape, dt, parts):
        a = self.t[0:parts, off // 2:(off + nb) // 2]
        if dt != BF16:
            a = a.bitcast(dt)
        if len(shape) == 1:
            return a
        names = ' '.join('d%d' % i for i in range(len(shape)))
        kw = {'d%d' % i: shape[i] for i in range(1, len(shape))}
        return a.rearrange('p (%s) -> p %s' % (names, names), **kw)

    @staticmethod
    def _nb(shape, dt):
        n = 1
        for s in shape:
            n *= s
        return n * (2 if dt == BF16 else 4)

    def alloc(self, shape, dt=BF16, parts=128):
        off = (self.off + 31) // 32 * 32
        nb = self._nb(shape, dt)
        assert off + nb <= self.top, ("arena overflow", off, nb, self.top)
        self.off = off + nb
        self.peak = max(self.peak, self.off + (self.cap - self.top))
        return self._view(off, nb, shape, dt, parts)

    def alloc_top(self, shape, dt=BF16, parts=128):
        nb = self._nb(shape, dt)
        off = (self.top - nb) // 32 * 32
        assert off >= self.off, ("arena overflow (top)", off, nb, self.off)
        self.top = off
        self.peak = max(self.peak, self.off + (self.cap - self.top))
        return self._view(off, nb, shape, dt, parts)


def build_full(nc, dbg=None, stop_after=None, att_limit=(4, 3)):
    es = ExitStack()
    E = es.enter_context

    def din(name, shape, dt=F32):
        return nc.dram_tensor(name, list(shape), dt, kind="ExternalInput").ap()
    xe = din("xe", [EXT, D])
    pe = din("pe", [OWN, 256])
    ident_d = din("ident", [128, 128])
    cmask_d = din("cmask", [128, 3, 256])
    perm_d = din("permT", [128, 128])
    shift_d = din("shiftD", [128, 64])
    kp_d = din("kp", [64, NKP])
    rot_n = din("rotn", [96, 2, OWN17 + 2048])
    valid_d = [din("valid%d" % g, [128, 64]) for g in range(3)]
    w_in = din("w_in", [D, 4864])
    g_mix = din("g_mix", [128, 8])
    gate_b = din("gate_b", [128, 16])
    lam_re = din("lam_re", [64, 32])
    lam_im = din("lam_im", [64, 32])
    log_dt = din("log_dt", [64, 32])
    b_re = din("b_re", [64, 32, 16])
    b_im = din("b_im", [64, 32, 16])
    c_re = din("c_re", [64, 32, 16])
    c_im = din("c_im", [64, 32, 16])
    d_col = din("d_col", [128, 32])
    glu_w = din("glu_w", [512, 512])
    glu_b = din("glu_b", [128, 4])
    w_ba = din("w_ba", [512, D])
    w_bb = din("w_bb", [256, D])
    w_out = din("w_out", [D, D])
    g_ffn = din("g_ffn", [128, 8])
    w_gate = din("w_gate", [D, DFF])
    w_up = din("w_up", [D, DFF])
    conv_w = din("conv_w", [128, NF, 3])
    conv_b = din("conv_b", [128, NF])
    w_down = din("w_down", [DFF, D])
    g_ple = din("g_ple", [128, 8])
    ple_wg = din("ple_wg", [D, D])
    ple_wp = din("ple_wp", [256, D])
    g_fin = din("g_fin", [128, D])
    out_d = nc.dram_tensor("out", [OWN, D], F32, kind="ExternalOutput").ap()
    dbg = dbg or {}
    dbg_d = {k: nc.dram_tensor("dbg_" + k, list(shp), F32, kind="ExternalOutput").ap() for k, shp in dbg.items()}

    CAP = 207 * 1024
    arena_t = E(nc.sbuf_tensor("arena", [128, CAP // 2], BF16))
    A = Arena(arena_t, CAP)
    ps = [E(nc.psum_tensor("ps%d" % i, [128, 512], F32)) for i in range(8)]
    S = Sched(nc, es)

    def PS(i, parts=128):
        return ps[i][0:parts, :]

    def PSB(i, parts=128):
        return ps[i][0:parts, :].bitcast(BF16)

    def pk(i):
        return ('ps', i)

    def TT(eng, out, in0, in1, op, r, w):
        S.op(eng, lambda e: e.tensor_tensor(out=out, in0=in0, in1=in1, op=op), r, w)

    def TS(eng, out, in0, s1, s2, op0, op1, r, w):
        if op1 is None:
            S.op(eng, lambda e: e.tensor_scalar(out=out, in0=in0, scalar1=s1, scalar2=None, op0=op0), r, w)
        else:
            S.op(eng, lambda e: e.tensor_scalar(out=out, in0=in0, scalar1=s1, scalar2=s2, op0=op0, op1=op1), r, w)

    def STT(eng, out, in0, scalar, in1, op0, op1, r, w):
        eng = 'vector'
        S.op(eng, lambda e: e.scalar_tensor_tensor(out=out, in0=in0, scalar=scalar, in1=in1, op0=op0, op1=op1), r, w)

    def ACT(out, in_, func, r, w, bias=None, scale=None, accum=None):
        kw = {}
        if bias is not None:
            kw['bias'] = bias
        if scale is not None:
            kw['scale'] = scale
        if accum is not None:
            kw['accum_out'] = accum
        S.op('scalar', lambda e: e.activation(out=out, in_=in_, func=func, **kw), r, w)

    def CP(eng, out, in_, r, w):
        if eng == 'scalar':
            ACT(out, in_, AF.Copy, r, w)
        else:
            S.op(eng, lambda e: e.tensor_copy(out=out, in_=in_), r, w)

    def MMS(groups, r, w):
        def f(e):
            ins = None
            for out, pairs in groups:
                n = len(pairs)
                for i, (l, rh) in enumerate(pairs):
                    ins = e.matmul(out, lhsT=l, rhs=rh, start=(i == 0), stop=(i == n - 1))
            return ins
        S.op('tensor', f, r, w)

    def MMRAW(steps, r, w):
        def f(e):
            ins = None
            for out, l, rh, st, sp in steps:
                ins = e.matmul(out, lhsT=l, rhs=rh, start=st, stop=sp)
            return ins
        S.op('tensor', f, r, w)

    def TRS(items, r, w):
        def f(e):
            ins = None
            for out, in_, npart in items:
                ins = e.transpose(out=out, in_=in_, identity=ident[0:npart, 0:npart])
            return ins
        S.op('tensor', f, list(r) + ['ident'], w)

    def _zero_psum():
        for i_ in range(8):
            S.op('vector', lambda e, i_=i_: e.memset(ps[i_][:, :], 0.0), [], [pk(i_)])
    S.zero_psum = _zero_psum
    flip = [0]

    def evac_eng():
        flip[0] ^= 1
        return 'vector' if flip[0] else 'scalar'

    def wload(dst, src, key):
        S.dma('gpsimd', dst, src, writes=[key])

    def dbg_out(name, ap_f32, key):
        if name in dbg_d:
            S.dma('sync', dbg_d[name], ap_f32, reads=[key])

    ident_f = A.alloc_top([128], F32)
    ident = A.alloc_top([128], BF16)
    cmask = A.alloc_top([3, 256], BF16)
    permT = A.alloc_top([128], BF16)
    shiftD = A.alloc_top([64], F32)
    S.dma('sync', shiftD, shift_d, writes=['shiftD'])
    gm = A.alloc_top([8], F32)
    eps_c = A.alloc_top([1], F32)
    S.op('gpsimd', lambda e: e.memset(eps_c, EPS), [], ['eps_c'])
    m0 = A.mark()
    cmask_f = A.alloc([3, 256], F32)
    perm_f = A.alloc([128], F32)
    S.dma('sync', perm_f, perm_d, writes=['perm_f'])
    CP('vector', permT, perm_f, ['perm_f'], ['permT'])
    S.dma('sync', ident_f, ident_d, writes=['ident_f'])
    S.dma('sync', cmask_f, cmask_d, writes=['cmask_f'])
    S.dma('sync', gm, g_mix, writes=['gm'])
    CP('vector', ident, ident_f, ['ident_f'], ['ident'])
    CP('vector', cmask, cmask_f, ['cmask_f'], ['cmask'])
    S.barrier()
    A.release(m0)

    def norm_stats(src, src_key, scr):
        sq, xb, st, k = scr['sq'], scr['xb'], scr['st'], scr['key']
        ACT(sq, src, AF.Square, [src_key], [k + 'sq', k + 'st'], accum=st[:, 0:1])
        ACT(st[:, 2:3], st[:, 0:1], AF.Ln, [k + 'st', 'eps_c'], [k + 'st'], bias=eps_c[:, 0:1], scale=1.0 / D)
        ACT(st[:, 3:4], st[:, 2:3], AF.Exp, [k + 'st'], [k + 'st'], scale=-0.5)
        ACT(xb, src, AF.Copy, [src_key, k + 'st'], [k + 'xb'], scale=st[:, 3:4])

    def norm_tr(dstT, dst_key, gfm, gkey, scr, psb):
        xb, k = scr['xb'], scr['key']
        TRS([(PSB(psb)[:, kk * 128:(kk + 1) * 128], xb[:, kk * 128:(kk + 1) * 128], 128) for kk in range(8)],
            [k + 'xb'], [pk(psb)])
        TT('vector', dstT, PSB(psb).rearrange("p (k n) -> p k n", k=8), gfm.unsqueeze(2).to_broadcast([128, 8, 128]),
           ALU.mult, [pk(psb), gkey], [dst_key])

    def norm_T(src, src_key, dstT, dst_key, gfm, gkey, scr, psb):
        norm_stats(src, src_key, scr)
        norm_tr(dstT, dst_key, gfm, gkey, scr, psb)

    junk_sq = []

    def alloc_nscr(key, share_sq=False):
        if not (share_sq and junk_sq):
            junk_sq.append(A.alloc([D], BF16))
        return {'sq': junk_sq[-1], 'xb': A.alloc([D], BF16), 'st': A.alloc([4], F32), 'key': key}

    topA = A.mark_top()
    xnA = A.alloc_top([8, OWN], BF16)
    xnB = A.alloc_top([8, OWN17], BF16)
    a_mark = A.mark()

    lr = A.alloc([32], F32, 64); li = A.alloc([32], F32, 64); ldt = A.alloc([32], F32, 64)
    kp = A.alloc([NKP], F32, 64)
    PWr = A.alloc([32, NKP], F32, 64); PWi = A.alloc([32, NKP], F32, 64)
    Cr = A.alloc([32, 16], F32, 64); Ci = A.alloc([32, 16], F32, 64)
    Bbr = A.alloc([32, 16], F32, 64); Bbi = A.alloc([32, 16], F32, 64)
    Aa = A.alloc([32, 2], F32, 64); Ab = A.alloc([32, 2], F32, 64)
    dcol = A.alloc([32], F32)
    UT = A.alloc([32, 2, 136], BF16)
    un_ = A.alloc([32 * 2 * 136], BF16)
    SPb = un_[0:64, :].rearrange("p (g r c) -> p g r c", g=32, r=2)
    FT = un_[:, 0:32 * 2 * 2 * 64].rearrange("p (g b r n) -> p g b r n", g=32, b=2, r=2)
    y_mark = A.mark()
    TG = 'vector'
    nscr_l = [alloc_nscr('n1a'), alloc_nscr('n1b', share_sq=True)]
    xt_buf = [A.alloc([D], F32) for _ in range(2)]

    def piece_cfg(pc):
        ntile = 17 if pc == 3 else 16
        xn = xnB if pc in (1, 3) else xnA
        xk = 'xnB' if pc in (1, 3) else 'xnA'
        return ntile, xn, xk

    def norm_tile_a(pc, t):
        ntile, xn, xk = piece_cfg(pc)
        if t >= ntile:
            return
        et = pc * 16 + t
        xb_ = xt_buf[t % 2]
        xkey = 'xt%d' % (t % 2)
        S.dma('sync', xb_, xe[et * 128:(et + 1) * 128, :], writes=[xkey])
        norm_stats(xb_, xkey, nscr_l[t % 2])

    def norm_tile_b(pc, t):
        ntile, xn, xk = piece_cfg(pc)
        if t < 0 or t >= ntile:
            return
        norm_tr(xn[:, :, t * 128:(t + 1) * 128], xk, gm, 'gm', nscr_l[t % 2], 4 + t % 2)

    def norm_tile(pc, t):
        norm_tile_a(pc, t)
        norm_tile_b(pc, t)

    wssm = A.alloc([8, 512], BF16)
    wload(wssm, w_in[:, 0:512].rearrange("(k p) n -> p k n", p=128), 'wssm')
    Ucm = A.alloc([32, 16, 16], BF16)
    OWN_CTS = [(0, 68), (68, 68)]
    for t in range(16):
        norm_tile(0, t)
    pending_red = []

    def piece_part1(pc):
        ntile, xn, xk = piece_cfg(pc)
        nch = ntile * 8
        cts = OWN_CTS if pc == 3 else [(0, 128)]
        nt_next = 0
        for (c0, ncc) in cts:
            for tau in range(16):
                pb = tau % 2
                lo = c0 * 16 + tau
                MMS([(PS(pb, ncc), [(xn[:, k, lo:lo + 16 * (ncc - 1) + 1:16], wssm[:, k, :]) for k in range(8)])],
                    [xk, 'wssm'], [pk(pb)])
                CP(evac_eng(), Ucm[0:ncc, :, tau, :], PS(pb, ncc).rearrange("p (g h) -> p g h", g=32), [pk(pb)], ['Ucm'])
                if pc < 3 and nt_next < 18:
                    norm_tile_b(pc + 1, nt_next - 1)
                    norm_tile_a(pc + 1, nt_next)
                    nt_next += 1
                if pending_red:
                    pending_red.pop(0)()
            for g in range(32):
                pb = 2 + g % 2
                TRS([(PSB(pb)[:, b * ncc:(b + 1) * ncc], Ucm[0:ncc, g, 8 * b:8 * b + 8, :].rearrange("p s h -> p (s h)"), ncc) for b in range(2)],
                    ['Ucm'], [pk(pb)])
                CP(evac_eng(), UT[:, g, :, c0:c0 + ncc], PSB(pb)[:, 0:2 * ncc].rearrange("p (b c) -> p b c", b=2),
                   [pk(pb)], ['UT'])
        while pc < 3 and nt_next < 18:
            norm_tile_b(pc + 1, nt_next - 1)
            norm_tile_a(pc + 1, nt_next)
            nt_next += 1

    piece_part1(0)
    par_mark = A.mark()
    for dst, src, key in ((lr, lam_re, 'lr'), (li, lam_im, 'li'), (ldt, log_dt, 'ldt'), (kp, kp_d, 'kp'),
                          (Cr, c_re, 'Cr'), (Ci, c_im, 'Ci'), (dcol, d_col, 'dcol')):
        S.dma('sync', dst, src, writes=[key])
    Br = A.alloc([32, 16], F32, 64); Bi = A.alloc([32, 16], F32, 64)
    S.dma('sync', Br, b_re, writes=['Br']); S.dma('sync', Bi, b_im, writes=['Bi'])
    dt_ = A.alloc([32], F32, 64); lrdt = A.alloc([32], F32, 64); lidt = A.alloc([32], F32, 64)
    ACT(dt_, ldt, AF.Exp, ['ldt'], ['dt'])
    TT(TG, lrdt, lr, dt_, ALU.mult, ['lr', 'dt'], ['lrdt'])
    TT(TG, lidt, li, dt_, ALU.mult, ['li', 'dt'], ['lidt'])
    SH = [64, 32, NKP]
    argm = A.alloc([32, NKP], F32, 64); ang = A.alloc([32, NKP], F32, 64)
    tq = A.alloc([32, NKP], F32, 64); ti = A.alloc([32, NKP], I32, 64); tk = argm
    mag = A.alloc([32, NKP], F32, 64); sn = A.alloc([32, NKP], F32, 64); cs = sn
    kpb = kp.unsqueeze(1).to_broadcast(SH)
    TT(TG, argm, lrdt.unsqueeze(2).to_broadcast(SH), kpb, ALU.mult, ['lrdt', 'kp'], ['argm'])
    ACT(mag, argm, AF.Exp, ['argm'], ['mag'])
    TT(TG, ang, lidt.unsqueeze(2).to_broadcast(SH), kpb, ALU.mult, ['lidt', 'kp'], ['ang'])

    def sin_rr(dst, shift, key):
        R_, W_ = ['ang', 'rr'], ['rr', 'argm']
        TS(TG, tq, ang, shift, 1.0 / TWO_PI, ALU.add, ALU.mult, R_, W_)
        CP(TG, ti, tq, R_, W_)
        CP(TG, tk, ti, R_, W_)
        TS(TG, tq, ang, shift, None, ALU.add, None, R_, W_)
        STT(TG, tq, tk, -6.28125, tq, ALU.mult, ALU.add, R_, W_)
        STT(TG, tq, tk, -(TWO_PI - 6.28125), tq, ALU.mult, ALU.add, R_, W_)
        TS(TG, tk, tq, PI, -TWO_PI, ALU.is_gt, ALU.mult, R_, W_)
        TT(TG, tq, tq, tk, ALU.add, R_, W_)
        TS(TG, tk, tq, -PI, TWO_PI, ALU.is_lt, ALU.mult, R_, W_)
        TT(TG, tq, tq, tk, ALU.add, R_, W_)
        TS(TG, tq, tq, PI, -PI, ALU.min, ALU.max, R_, W_)
        ACT(dst, tq, AF.Sin, ['rr'], [key])
    sin_rr(sn, 0.0, 'sn')
    TT(TG, PWi, mag, sn, ALU.mult, ['mag', 'sn'], ['PW'])
    sin_rr(cs, PI / 2, 'sn')
    TT(TG, PWr, mag, cs, ALU.mult, ['mag', 'sn'], ['PW'])
    ar = PWr[:, :, 8]; ai = PWi[:, :, 8]
    t1 = A.alloc([32], F32, 64); t2 = A.alloc([32], F32, 64); t3 = A.alloc([32], F32, 64)
    den = A.alloc([32], F32, 64); cr_ = A.alloc([32], F32, 64); ci_ = A.alloc([32], F32, 64); arm1 = A.alloc([32], F32, 64)
    RC, WC = ['PW', 'lr', 'li', 'cf'], ['cf']
    TT(TG, t1, lr, lr, ALU.mult, RC, WC)
    TT(TG, t2, li, li, ALU.mult, RC, WC)
    TT(TG, den, t1, t2, ALU.add, RC, WC)
    S.op('vector', lambda e: e.reciprocal(out=den, in_=den), RC, WC)
    TS(TG, arm1, ar, -1.0, None, ALU.add, None, RC, WC)
    TT(TG, t1, arm1, lr, ALU.mult, RC, WC)
    TT(TG, t2, ai, li, ALU.mult, RC, WC)
    TT(TG, t3, t1, t2, ALU.add, RC, WC)
    TT(TG, cr_, t3, den, ALU.mult, RC, WC)
    TT(TG, t1, ai, lr, ALU.mult, RC, WC)
    TT(TG, t2, arm1, li, ALU.mult, RC, WC)
    TT(TG, t3, t1, t2, ALU.subtract, RC, WC)
    TT(TG, ci_, t3, den, ALU.mult, RC, WC)
    u1 = tq.rearrange("p g k -> p (g k)")[:, 0:512].rearrange("p (g h) -> p g h", g=32)
    u2 = ang.rearrange("p g k -> p (g k)")[:, 0:512].rearrange("p (g h) -> p g h", g=32)
    SB_ = [64, 32, 16]
    crb = cr_.unsqueeze(2).to_broadcast(SB_); cib = ci_.unsqueeze(2).to_broadcast(SB_)
    RB, WB = ['cf', 'Br', 'Bi', 'bb', 'PW'], ['bb']
    TT(TG, u1, Br, crb, ALU.mult, RB, WB)
    TT(TG, u2, Bi, cib, ALU.mult, RB, WB)
    TT(TG, Bbr, u1, u2, ALU.subtract, RB, WB)
    TT(TG, u1, Bi, crb, ALU.mult, RB, WB)
    TT(TG, u2, Br, cib, ALU.mult, RB, WB)
    TT(TG, Bbi, u1, u2, ALU.add, RB, WB)
    CP(TG, Aa, PWr[:, :, 23:24].to_broadcast([64, 32, 2]), RB, WB)
    TS(TG, Ab[:, :, 0], PWi[:, :, 23], -1.0, None, ALU.mult, None, RB, WB)
    CP(TG, Ab[:, :, 1], PWi[:, :, 23], RB, WB)
    S.barrier()
    A.release(par_mark)

    GB = 8

    def alloc_tb():
        return dict(Hr=A.alloc([GB, 8, 16], F32, 64), Hi=A.alloc([GB, 8, 16], F32, 64),
                    Hrb=A.alloc([GB, 128], BF16, 64), Hib=A.alloc([GB, 128], BF16, 64),
                    w1=A.alloc([GB, 16, 16], F32, 64), w2=A.alloc([GB, 16, 16], F32, 64))

    RT, WT = ['PW', 'bb', 'Cr', 'Ci', 'tb'], ['tb']

    def gen_H(tb, g0):
        gs = slice(g0, g0 + GB)
        SHH = [64, GB, 8, 16]
        pwr_h = PWr[:, gs, 0:8].unsqueeze(3).to_broadcast(SHH)
        pwi_h = PWi[:, gs, 0:8].unsqueeze(3).to_broadcast(SHH)
        bbr_h = Bbr[:, gs, :].unsqueeze(2).to_broadcast(SHH)
        bbi_h = Bbi[:, gs, :].unsqueeze(2).to_broadcast(SHH)
        w1h = tb['w1'][:, :, 0:8, :]; w2h = tb['w2'][:, :, 0:8, :]
        Hr, Hi = tb['Hr'], tb['Hi']
        TT(TG, w1h, bbr_h, pwr_h, ALU.mult, RT, WT)
        TT(TG, w2h, bbi_h, pwi_h, ALU.mult, RT, WT)
        TT(TG, Hr, w1h, w2h, ALU.subtract, RT, WT)
        TT(TG, w1h, bbi_h, pwr_h, ALU.mult, RT, WT)
        TT(TG, w2h, bbr_h, pwi_h, ALU.mult, RT, WT)
        TT(TG, Hi, w1h, w2h, ALU.add, RT, WT)
        CP(TG, tb['Hrb'].rearrange("p g (s h) -> p g s h", s=8), Hr, RT, WT)
        CP(TG, tb['Hib'].rearrange("p g (s h) -> p g s h", s=8), Hi, RT, WT)

    def gen_F(tb, Fpm, g0):
        gs = slice(g0, g0 + GB)
        SHH = [64, GB, 8, 16]
        w1h = tb['w1'][:, :, 0:8, :]; w2h = tb['w2'][:, :, 0:8, :]
        Hr, Hi = tb['Hr'], tb['Hi']
        for b, idx in ((0, 23), (1, 15)):
            pr = PWr[:, gs, idx:idx + 1].unsqueeze(3).to_broadcast(SHH)
            pi_ = PWi[:, gs, idx:idx + 1].unsqueeze(3).to_broadcast(SHH)
            fre = Fpm[:, :, b, 0, :].rearrange("p g (s h) -> p g s h", s=8)
            fim = Fpm[:, :, b, 1, :].rearrange("p g (s h) -> p g s h", s=8)
            TT(TG, w1h, Hr, pr, ALU.mult, RT, WT)
            TT(TG, w2h, Hi, pi_, ALU.mult, RT, WT)
            TT(TG, fre, w1h, w2h, ALU.subtract, RT, WT)
            TT(TG, w1h, Hi, pr, ALU.mult, RT, WT)
            TT(TG, w2h, Hr, pi_, ALU.mult, RT, WT)
            TT(TG, fim, w1h, w2h, ALU.add, RT, WT)
        for gl in range(GB):
            g = g0 + gl
            pb2 = 2 + g % 2
            TRS([(PSB(pb2)[:, (b * 2 + ri) * 64:(b * 2 + ri + 1) * 64], Fpm[:, gl, b, ri, :], 64)
                 for b in range(2) for ri in range(2)], ['tb'], [pk(pb2)])
            CP('scalar', FT[:, g, :, :, :].rearrange("p b r n -> p (b r n)"), PSB(pb2)[:, 0:256], [pk(pb2)], ['FT'])

    def gen_E(tb, Etb, g0, ek='Et'):
        gs = slice(g0, g0 + GB)
        SHE = [64, GB, 16, 16]
        w1, w2 = tb['w1'], tb['w2']
        pre = PWr[:, gs, 8:24].unsqueeze(3).to_broadcast(SHE)
        pie = PWi[:, gs, 8:24].unsqueeze(3).to_broadcast(SHE)
        cre = Cr[:, gs, :].unsqueeze(2).to_broadcast(SHE)
        cie = Ci[:, gs, :].unsqueeze(2).to_broadcast(SHE)
        ere = Etb[:, :, 0, :].rearrange("p g (t h) -> p g t h", t=16)
        eim = Etb[:, :, 1, :].rearrange("p g (t h) -> p g t h", t=16)
        TT(TG, w1, cre, pre, ALU.mult, RT, WT)
        TT(TG, w2, cie, pie, ALU.mult, RT, WT)
        TT(TG, ere, w1, w2, ALU.subtract, RT, ['tb', ek])
        TT(TG, w1, cre, pie, ALU.mult, RT, WT)
        TT(TG, w2, cie, pre, ALU.mult, RT, WT)
        STT(TG, eim, w1, -1.0, w2, ALU.mult, ALU.subtract, RT, ['tb', ek])

    def gen_Dmm(tb, Dtb, Etb, g0, ek='Et', dk='Dt'):
        for gl in range(GB):
            g = g0 + gl
            pb = g % 2
            MMS([(PS(pb)[:, 0:256], [(tb['Hrb'][:, gl, :], Etb[:, gl, 0, :]), (tb['Hib'][:, gl, :], Etb[:, gl, 1, :])])],
                ['tb', ek], [pk(pb)])
            TT('vector', Dtb[:, gl, :], PS(pb)[:, 0:256], cmask[:, 2, :], ALU.mult, [pk(pb), 'cmask'], [dk])
            STT('vector', Dtb[:, gl, 0:128], ident_f, dcol[:, g:g + 1], Dtb[:, gl, 0:128], ALU.mult, ALU.add,
                [dk, 'ident_f', 'dcol'], [dk])

    def gen_DE(tb, Dtb, Etb, g0):
        gen_E(tb, Etb, g0)
        gen_Dmm(tb, Dtb, Etb, g0)

    tb_mark = A.mark()
    tb = alloc_tb()
    Fpm = A.alloc([GB, 2, 2, 128], BF16, 64)
    for gb in range(4):
        gen_H(tb, gb * GB)
        gen_F(tb, Fpm, gb * GB)
    if stop_after == 'tables':
        Dtb = A.alloc([GB, 256], BF16); Etb = A.alloc([GB, 2, 256], BF16, 64)
        gen_H(tb, 8)
        gen_DE(tb, Dtb, Etb, 8)
        Dtf = A.alloc([GB, 256], F32)
        CP('vector', Dtf, Dtb, ['Dt'], ['Dtf'])
        dbg_out('dt', Dtf, 'Dtf')
        FTf = A.alloc([32, 2, 2, 64], F32)
        CP('vector', FTf, FT, ['FT'], ['FTf'])
        dbg_out('ft', FTf, 'FTf')
        dbg_out('pwr', PWr, 'PW'); dbg_out('pwi', PWi, 'PW')
        return finish(nc, es, S)
    S.barrier()
    A.release(tb_mark)

    rt1 = A.alloc([32, 2], F32, 64); rt2 = A.alloc([32, 2], F32, 64)
    redA = A.alloc([4, 16, 8], F32, 64); redB = A.alloc([4, 16, 8], F32, 64)
    Yk = A.alloc([32, 2, 16], F32, 64); Zc = A.alloc([32, 2], F32, 64)
    SV = A.alloc([32, 2, 137], F32, 64)
    S.op('gpsimd', lambda e: e.memset(SV[:, :, :, 0:1], 0.0), [], ['SVc'])

    def piece_part2(pc):
        ntile, xn, xk = piece_cfg(pc)
        nch = ntile * 8
        while pending_red:
            pending_red.pop(0)()
        for g in range(32):
            pb = 6 + g % 2
            MMS([(PS(pb, 64)[:, ri * nch:(ri + 1) * nch], [(FT[:, g, b, ri, :], UT[:, g, b, 0:nch]) for b in range(2)])
                 for ri in range(2)], ['FT', 'UT'], [pk(pb)])
            CP(evac_eng(), SV[:, g, :, 1:1 + nch], PS(pb, 64)[:, 0:2 * nch].rearrange("p (r c) -> p r c", r=2),
               [pk(pb)], ['SV'])
        if pc < 3:
            def red_quarter(gq):
                gs = slice(4 * gq, 4 * gq + 4)
                SH5 = [64, 4, 16, 8]
                vr = SV[:, gs, 0, 1:129].rearrange("p g (k j) -> p g k j", j=8)
                vi = SV[:, gs, 1, 1:129].rearrange("p g (k j) -> p g k j", j=8)
                w1r = PWr[:, gs, 24:32].unsqueeze(2).to_broadcast(SH5)
                w1i = PWi[:, gs, 24:32].unsqueeze(2).to_broadcast(SH5)
                RR, WR = ['SV', 'PW', 'red'], ['red']
                TT('vector', redA, vr, w1r, ALU.mult, RR, WR)
                TT('vector', redB, vi, w1i, ALU.mult, RR, WR)
                TT('vector', redA, redA, redB, ALU.subtract, RR, WR)
                S.op('vector', lambda e: e.tensor_reduce(out=Yk[:, gs, 0, :], in_=redA, axis=mybir.AxisListType.X, op=ALU.add), RR, ['red', 'Yk'])
                TT('vector', redA, vr, w1i, ALU.mult, RR, WR)
                TT('vector', redB, vi, w1r, ALU.mult, RR, WR)
                TT('vector', redA, redA, redB, ALU.add, RR, WR)
                S.op('vector', lambda e: e.tensor_reduce(out=Yk[:, gs, 1, :], in_=redA, axis=mybir.AxisListType.X, op=ALU.add), RR, ['red', 'Yk'])

            def red_outer():
                y_r = Yk[:, :, 0, :]; y_i = Yk[:, :, 1, :]
                w2r = PWr[:, :, 32:48]; w2i = PWi[:, :, 32:48]
                r2a = redA.rearrange("p g k j -> p (g k j)")[:, 0:512].rearrange("p (g k) -> p g k", g=32)
                r2b = redB.rearrange("p g k j -> p (g k j)")[:, 0:512].rearrange("p (g k) -> p g k", g=32)
                RR, WR = ['Yk', 'PW', 'red'], ['red']
                TT('vector', r2a, y_r, w2r, ALU.mult, RR, WR)
                TT('vector', r2b, y_i, w2i, ALU.mult, RR, WR)
                TT('vector', r2a, r2a, r2b, ALU.subtract, RR, WR)
                S.op('vector', lambda e: e.tensor_reduce(out=Zc[:, :, 0], in_=r2a, axis=mybir.AxisListType.X, op=ALU.add), RR, ['red', 'Zc'])
                TT('vector', r2a, y_r, w2i, ALU.mult, RR, WR)
                TT('vector', r2b, y_i, w2r, ALU.mult, RR, WR)
                TT('vector', r2a, r2a, r2b, ALU.add, RR, WR)
                S.op('vector', lambda e: e.tensor_reduce(out=Zc[:, :, 1], in_=r2a, axis=mybir.AxisListType.X, op=ALU.add), RR, ['red', 'Zc'])
                Xin = SV[:, :, :, 0]
                a128r = PWr[:, :, 48]; a128i = PWi[:, :, 48]
                RC_, WC_ = ['Zc', 'SVc', 'PW', 'rt'], ['rt']
                TT('vector', rt1[:, :, 0], Xin[:, :, 0], a128r, ALU.mult, RC_, WC_)
                TT('vector', rt1[:, :, 1], Xin[:, :, 1], a128i, ALU.mult, RC_, WC_)
                TT('vector', rt2[:, :, 0], Xin[:, :, 1], a128r, ALU.mult, RC_, WC_)
                TT('vector', rt2[:, :, 1], Xin[:, :, 0], a128i, ALU.mult, RC_, WC_)
                TT('vector', rt1[:, :, 0], rt1[:, :, 0], rt1[:, :, 1], ALU.subtract, RC_, WC_)
                TT('vector', rt2[:, :, 0], rt2[:, :, 0], rt2[:, :, 1], ALU.add, RC_, WC_)
                TT('vector', SV[:, :, 0, 0], Zc[:, :, 0], rt1[:, :, 0], ALU.add, RC_, ['SVc'])
                TT('vector', SV[:, :, 1, 0], Zc[:, :, 1], rt2[:, :, 0], ALU.add, RC_ + ['SVc'], ['SVc'])

            for gq in range(8):
                pending_red.append(lambda gq=gq: red_quarter(gq))
            pending_red.append(red_outer)
        else:
            while pending_red:
                pending_red.pop(0)()
            for c in range(nch):
                Xp = SV[:, :, :, c]
                Xn = SV[:, :, :, c + 1]
                kin = ['SV', 'SVc', 'bb']
                TT('vector', rt1, Xp, Aa, ALU.mult, kin, ['rt1'])
                TT('vector', rt2[:, :, 0], Xp[:, :, 1], Ab[:, :, 0], ALU.mult, kin, ['rt2a'])
                TT('vector', rt2[:, :, 1], Xp[:, :, 0], Ab[:, :, 1], ALU.mult, kin, ['rt2b'])
                TT('vector', Xn, Xn, rt1, ALU.add, ['rt1', 'SV'], ['SV'])
                TT('vector', Xn, Xn, rt2, ALU.add, ['rt2a', 'rt2b', 'SV'], ['SV'])

    piece_part2(0)
    for pc in range(1, 4):
        piece_part1(pc)
        piece_part2(pc)
    CP('vector', SPb, SV[:, :, :, 0:136], ['SV', 'SVc', 'FT'], ['SPb', 'FT'])
    if 'sv' in dbg_d:
        dbg_out('sv', SV, 'SV')
    S.barrier()
    A.release(y_mark)

    yT = A.alloc_top([4, OWN17], BF16)
    tb = alloc_tb()
    Dtb2 = [A.alloc([GB, 256], BF16) for _ in range(2)]
    Etb2 = [A.alloc([GB, 2, 256], BF16, 64) for _ in range(2)]
    Yg = A.alloc([2, 16, 128], BF16)
    gen_H(tb, 0)
    gen_E(tb, Etb2[0], 0, ek=('Et', 0))
    gen_Dmm(tb, Dtb2[0], Etb2[0], 0, ek=('Et', 0), dk=('Dt', 0))
    for gb in range(4):
        g0 = gb * GB
        Dtb, Etb = Dtb2[gb % 2], Etb2[gb % 2]
        dk_, ek_ = ('Dt', gb % 2), ('Et', gb % 2)
        if gb + 1 < 4:
            nb_ = (gb + 1) % 2
            gen_H(tb, g0 + GB)
            gen_E(tb, Etb2[nb_], g0 + GB, ek=('Et', nb_))
        for cti, (c0, ncc) in enumerate(OWN_CTS):
            for gp in range(4):
                pb = 4 + gp % 2
                steps = []
                for j in range(2):
                    gl = gp * 2 + j
                    g = g0 + gl
                    o = PS(pb, ncc)[:, j * 256:(j + 1) * 256]
                    steps.append((o, UT[:, g, 0, c0:c0 + ncc], Dtb[:, gl, :], True, False))
                    steps.append((o[:, 128:256], UT[:, g, 1, c0:c0 + ncc], Dtb[:, gl, 0:128], False, False))
                    steps.append((o, SPb[:, g, 0, c0:c0 + ncc], Etb[:, gl, 0, :], False, False))
                    steps.append((o, SPb[:, g, 1, c0:c0 + ncc], Etb[:, gl, 1, :], False, True))
                MMRAW(steps, ['UT', dk_, 'SPb', ek_], [pk(pb)])
                ACT(Yg[0:ncc, cti, :, gp * 32:gp * 32 + 32].rearrange("p t (g h) -> p t g h", g=2),
                    PS(pb, ncc).rearrange("p (g t h) -> p t g h", g=2, t=16), AF.Gelu_apprx_tanh, [pk(pb)], ['Yg'])
        for cti, (c0, ncc) in enumerate(OWN_CTS):
            for th in range(2):
                pb = 6 + th
                TRS([(PSB(pb)[:, tt * ncc:(tt + 1) * ncc], Yg[0:ncc, cti, th * 8 + tt, :], ncc) for tt in range(8)],
                    ['Yg'], [pk(pb)])
                dst = yT[:, gb, :].rearrange("p (c t) -> p t c", t=16)[:, th * 8:th * 8 + 8, c0:c0 + ncc]
                CP(evac_eng(), dst, PSB(pb)[:, 0:8 * ncc].rearrange("p (t c) -> p t c", t=8), [pk(pb)], ['yT'])
        if gb + 1 < 4:
            gen_Dmm(tb, Dtb2[nb_], Etb2[nb_], g0 + GB, ek=('Et', nb_), dk=('Dt', nb_))
    S.barrier()
    A.release(a_mark)
    if 'yT' in dbg_d:
        yTf = A.alloc([4, OWN17], F32)
        CP('vector', yTf, yT, ['yT'], ['yTf'])
        dbg_out('yT', yTf, 'yTf')
        S.barrier()
        A.release(a_mark)
    if stop_after == 'ssm':
        return finish(nc, es, S)

    BLKS = [(0, 512), (512, 512), (1024, 512), (1536, 512), (2048, 128)]
    wglu = A.alloc([4, 512], BF16)
    gluB = A.alloc([4], F32)
    sg = A.alloc([4, 512], BF16)
    wload(wglu, glu_w.rearrange("(k p) n -> p k n", p=128), 'wglu')
    S.dma('sync', gluB, glu_b, writes=['gluB'])
    for bi, (o, n) in enumerate(BLKS):
        yk = ('yT', bi)
        for j in range(4):
            MMS([(PS(j)[:, 0:n], [(wglu[:, k, j * 128:(j + 1) * 128], yT[:, k, o:o + n]) for k in range(4)])],
                [yk, 'wglu'], [pk(j)])
        for j in range(4):
            ACT(sg[:, j, 0:n], PS(j)[:, 0:n], AF.Sigmoid, [pk(j), 'gluB'], [('sg', j)], bias=gluB[:, j:j + 1])
            TT('vector', yT[:, j, o:o + n], yT[:, j, o:o + n], sg[:, j, 0:n], ALU.mult, [('sg', j), yk], [yk])
    S.barrier()
    A.release(a_mark)
    if 'ya' in dbg_d:
        yTf = A.alloc([4, OWN17], F32)
        CP('vector', yTf, yT, [], ['yTf'])
        dbg_out('ya', yTf, 'yTf')
        S.barrier()
        A.release(a_mark)

    if stop_after == 'glu':
        return finish(nc, es, S)

    OT = A.alloc_top([4, OWN17], BF16, 64)
    acc = A.alloc([2, OWN17], F32)
    rden = A.alloc([512], F32, 64)
    wq = A.alloc([8, 128], BF16); wk = A.alloc([8, 128], BF16); wv = A.alloc([8, 128], BF16)
    wqp = True; wkp = True
    QT = A.alloc([OWN17], BF16)
    KT = A.alloc([OWN17 + 2048], BF16)
    VT = A.alloc([OWN17 + 2048], BF16)
    Vt3 = A.alloc([48, 3, 64], BF16)
    vld = A.alloc([64], F32)
    NRB = 3
    rtb = [A.alloc([2, 512], F32, 96) for _ in range(NRB)]
    rtmp = [A.alloc([512], F32, 96) for _ in range(2)]
    NPB = 4
    Pe = [A.alloc([2, 128], BF16) for _ in range(NPB)]
    Pm = [A.alloc([2, 128], BF16) for _ in range(NPB)]
    for pe_ in Pe:
        S.op('gpsimd', lambda e, pe_=pe_: e.memset(pe_, 0.0), [], [('Pe', Pe.index(pe_))])
    rcnt = [0]
    pcnt = [0]

    def proj_nat(dstT, dkey, col0, n, wmain, wperm, rhs_list, rcol0):
        i = rcnt[0] % 2
        ib = rcnt[0] % NRB
        rcnt[0] += 1
        pbm = 0 + i
        pbp = 2 + i
        MMS([(PS(pbm)[:, 0:n], [(wmain[:, k, :], rhs_list[k]) for k in range(8)])], ['xnA', 'xnB', 'watt'], [pk(pbm)])
        CP('scalar', dstT[:, col0:col0 + n], PS(pbm)[:, 0:n], [pk(pbm)], [dkey])
        if wperm is None:
            return
        MMS([(PS(pbp)[:, 0:n], [(permT, dstT[:, col0:col0 + n])])], [dkey, 'permT'], [pk(pbp)])
        S.dma('sync', rtb[ib][:, :, 0:n], rot_n[:, :, rcol0:rcol0 + n], writes=[('rtb', ib)])
        TT('vector', rtmp[i][:, 0:n], PS(pbm, 96)[:, 0:n], rtb[ib][:, 0, 0:n], ALU.mult, [pk(pbm), ('rtb', ib), dkey], [('rtmp', i)])
        TT('vector', rtb[ib][:, 1, 0:n], PS(pbp, 96)[:, 0:n], rtb[ib][:, 1, 0:n], ALU.mult, [pk(pbp), ('rtb', ib)], [('rtb', ib)])
        TT('vector', dstT[0:96, col0:col0 + n], rtmp[i][:, 0:n], rtb[ib][:, 1, 0:n], ALU.add,
           [('rtmp', i), ('rtb', ib), dkey], [dkey])

    def chunks(total, step=512):
        o = 0
        while o < total:
            yield o, min(step, total - o)
            o += step

    for hp in range(2):
        for gi, dil in enumerate(DILS):
            hA = 4 * gi + 2 * hp
            per_q = OWN17 // dil
            per_k = per_q + 128
            nblk = (per_k + 127) // 128
            halo = 128 * dil
            cq = 512 + 64 * hA
            ck = 1280 + 64 * hA
            cv = 2048 + 64 * hA
            for dst, c0_ in ((wq, cq), (wk, ck), (wv, cv)):
                wload(dst, w_in[:, c0_:c0_ + 128].rearrange("(k p) n -> p k n", p=128), 'watt')
            S.dma('sync', vld, valid_d[gi], writes=['vld'])
            CP('vector', Vt3[:, 0:dil * nblk, 1, :], vld[:, 0:dil * nblk].unsqueeze(2).to_broadcast([128, dil * nblk, 64]),
               ['vld', 'Vt'], ['Vt'])
            for o, n in chunks(halo):
                rl = [xnA[:, k, OWN - halo + o:OWN - halo + o + n] for k in range(8)]
                proj_nat(KT, 'KT', o, n, wk, wkp, rl, 2048 - halo + o)
                proj_nat(VT, 'VT', o, n, wv, None, rl, 0)
            for o, n in chunks(OWN17):
                rl = [xnB[:, k, o:o + n] for k in range(8)]
                proj_nat(KT, 'KT', halo + o, n, wk, wkp, rl, 2048 + o)
                proj_nat(QT, 'QT', o, n, wq, wqp, rl, 2048 + o)
                proj_nat(VT, 'VT', halo + o, n, wv, None, rl, 0)
            blist = [(r, j) for j in range(nblk) for r in range(dil)]
            for b0 in range(0, len(blist), 8):
                grp = blist[b0:b0 + 8]
                pb = 4 + (pcnt[0] % 2)
                pcnt[0] += 1
                items = []
                for jj, (r, j) in enumerate(grp):
                    nkb = min(128, per_k - 128 * j)
                    st = r + dil * 128 * j
                    items.append((PSB(pb, nkb)[:, jj * 128:(jj + 1) * 128], VT[:, st:st + dil * (nkb - 1) + 1:dil], 128))
                TRS(items, ['VT'], [pk(pb)])
                nkbs = [min(128, per_k - 128 * j) for (r, j) in grp]
                ee = evac_eng()
                jj = 0
                while jj < len(grp):
                    je = jj
                    while je < len(grp) and nkbs[je] == nkbs[jj]:
                        je += 1
                    nkb = nkbs[jj]
                    CP(ee, Vt3[0:nkb, b0 + jj:b0 + je, 0:3:2, :],
                       PSB(pb, nkb)[:, 128 * jj:128 * je].rearrange("p (j u d) -> p j u d", u=2, d=64), [pk(pb)], ['Vt'])
                    jj = je
            tiles = [(hj, r, t) for hj in range(2) for r in range(dil) for t in range((per_q + 127) // 128)]

            def att_scores(hj, r, t, i):
                rows = slice(64 * hj, 64 * hj + 64)
                nq = min(128, per_q - 128 * t)
                nk2 = min(128, per_k - 128 * (t + 1))
                pbs = 4 + i
                q0 = r + dil * 128 * t
                qsl = QT[rows, q0:q0 + dil * (nq - 1) + 1:dil]
                k1 = r + dil * 128 * t
                k2 = r + dil * 128 * (t + 1)
                MMS([(PS(pbs)[:, 0:nq], [(KT[rows, k1:k1 + dil * 127 + 1:dil], qsl)]),
                     (PS(pbs, nk2)[:, 128:128 + nq], [(KT[rows, k2:k2 + dil * (nk2 - 1) + 1:dil], qsl)])],
                    ['KT', 'QT'], [pk(pbs)])

            def att_rest(hj, r, t, i, ip):
                nq = min(128, per_q - 128 * t)
                nk2 = min(128, per_k - 128 * (t + 1))
                pbs = 4 + i
                pbo = 6 + i
                ACT(Pe[ip], PS(pbs)[:, 0:256].rearrange("p (u n) -> p u n", u=2), AF.Exp, [pk(pbs)], [('Pe', ip)], scale=0.125)
                TT('vector', Pm[ip], Pe[ip], cmask[:, 0:2, 0:128], ALU.mult,
                   [('Pe', ip), 'cmask'], [('Pm', ip)])
                b1 = r * nblk + t
                b2 = b1 + 1
                vc = slice(64 * hj, 64 * hj + 64)
                MMS([(PS(pbo, 64)[:, 0:nq], [(Vt[:, b1, vc], Pm[ip][:, 0, 0:nq]), (Vt[0:nk2, b2, vc], Pm[ip][0:nk2, 1, 0:nq])]),
                     (PS(pbo, 64)[:, 128:128 + nq], [(VO[:, b1, :], Pm[ip][:, 0, 0:nq]), (VO[0:nk2, b2, :], Pm[ip][0:nk2, 1, 0:nq])])],
                    [('Pm', ip), 'Vt', 'VO'], [pk(pbo)])
                t0_ = r + dil * 128 * t
                dsta = acc[:, hj, :, t0_:t0_ + dil * (nq - 1) + 1:dil]
                srcp = PS(pbo, 64)[:, 0:256].rearrange("p (u n) -> p u n", u=2)[:, :, 0:nq]
                if gi == 0:
                    CP('vector', dsta, srcp, [pk(pbo)], ['acc'])
                else:
                    TT('vector', dsta, dsta, srcp, ALU.add, [pk(pbo), 'acc'], ['acc'])

            def att_exp(hj, r, t, i, ip):
                pbs = 4 + i
                nq = min(128, per_q - 128 * t)
                nk2 = min(128, per_k - 128 * (t + 1))
                if nq == 128 and nk2 == 128:
                    ACT(Pe[ip], PS(pbs)[:, 0:256].rearrange("p (u n) -> p u n", u=2), AF.Exp, [pk(pbs)], [('Pe', ip)], scale=0.125)
                else:
                    ACT(Pe[ip][:, 0, 0:nq], PS(pbs)[:, 0:nq], AF.Exp, [pk(pbs)], [('Pe', ip)], scale=0.125)
                    ACT(Pe[ip][0:nk2, 1, 0:nq], PS(pbs, nk2)[:, 128:128 + nq], AF.Exp, [pk(pbs), ('Pe', ip)], [('Pe', ip)], scale=0.125)
                TT('vector', Pm[ip], Pe[ip], cmask[:, 0:2, 0:128], ALU.mult,
                   [('Pe', ip), 'cmask'], [('Pm', ip)])

            def att_pv(hj, r, t, i, ip):
                nq = min(128, per_q - 128 * t)
                nk2 = min(128, per_k - 128 * (t + 1))
                pbo = 6 + i
                b1 = t * dil + r
                b2 = (t + 1) * dil + r
                vb1 = Vt3[:, b1, :, :].rearrange("p u d -> p (u d)")[:, 64 * hj:64 * hj + 128]
                vb2 = Vt3[0:nk2, b2, :, :].rearrange("p u d -> p (u d)")[:, 64 * hj:64 * hj + 128]
                MMS([(PS(pbo)[:, 0:nq], [(vb1, Pm[ip][:, 0, 0:nq]), (vb2, Pm[ip][0:nk2, 1, 0:nq])])],
                    [('Pm', ip), 'Vt'], [pk(pbo)])

            def att_acc(hj, r, t, i, ip):
                nq = min(128, per_q - 128 * t)
                pbo = 6 + i
                t0_ = r + dil * 128 * t
                dsta = acc[:, hj, t0_:t0_ + dil * (nq - 1) + 1:dil]
                srcp = PS(pbo)[:, 0:nq]
                if gi == 0:
                    CP('vector', dsta, srcp, [pk(pbo)], ['acc'])
                else:
                    TT('vector', dsta, dsta, srcp, ALU.add, [pk(pbo), 'acc'], ['acc'])

            NTL = len(tiles)
            att_scores(*tiles[0], 0)
            if NTL > 1:
                att_scores(*tiles[1], 1)
            att_exp(*tiles[0], 0, 0)
            for n_ in range(NTL):
                att_pv(*tiles[n_], n_ % 2, n_ % NPB)
                if n_ + 2 < NTL:
                    att_scores(*tiles[n_ + 2], n_ % 2)
                if n_ + 1 < NTL:
                    att_exp(*tiles[n_ + 1], (n_ + 1) % 2, (n_ + 1) % NPB)
                att_acc(*tiles[n_], n_ % 2, n_ % NPB)
        for hj in range(2):
            for bi_, (o, n) in enumerate(BLKS):
                pbf = bi_ % 2
                MMS([(PS(pbf, 64)[:, 0:n], [(shiftD, acc[:, hj, o:o + n])])], ['acc', 'shiftD'], [pk(pbf)])
                if hj == 0:
                    TS('vector', rden[:, 0:n], PS(pbf, 64)[:, 0:n], 1e-30, None, ALU.max, None, [pk(pbf)], ['rden'])
                    S.op('vector', lambda e, n=n: e.reciprocal(out=rden[:, 0:n], in_=rden[:, 0:n]), ['rden'], ['rden'])
                    TT('vector', OT[:, 2 * hp + hj, o:o + n], acc[0:64, hj, o:o + n], rden[:, 0:n], ALU.mult, ['acc', 'rden'], ['OT'])
                else:
                    TS('vector', rden[:, 0:n], acc[0:64, hj, o:o + n], 1e-30, None, ALU.max, None, ['acc'], ['rden'])
                    S.op('vector', lambda e, n=n: e.reciprocal(out=rden[:, 0:n], in_=rden[:, 0:n]), ['rden'], ['rden'])
                    TT('vector', OT[:, 2 * hp + hj, o:o + n], PS(pbf, 64)[:, 0:n], rden[:, 0:n], ALU.mult, [pk(pbf), 'rden'], ['OT'])
    S.barrier()
    A.release(a_mark)
    if 'att' in dbg_d:
        otf = A.alloc([4, OWN17], F32, 64)
        CP('vector', otf, OT, [], ['otf'])
        dbg_out('att', otf, 'otf')
        S.barrier()
        A.release(a_mark)

    if stop_after == 'att':
        return finish(nc, es, S)

    A.release_top(A.mark_top())
    mixT = A.alloc([8, OWN17], BF16)
    b_mark0 = A.mark()
    gateB = A.alloc([16], F32)
    S.dma('sync', gateB, gate_b, writes=['gateB'])
    wga = [A.alloc([8, 128], BF16) for _ in range(2)]
    wgb = [A.alloc([8, 128], BF16) for _ in range(2)]
    wba = [A.alloc([4, 128], BF16) for _ in range(2)]
    wbb = [A.alloc([4, 128], BF16, 64) for _ in range(2)]
    sga = A.alloc([512], F32); sgb = A.alloc([512], F32)
    ta = A.alloc([512], F32); tbb_ = A.alloc([512], F32)
    for m in range(8):
        wi = m % 2
        wk_ = ('wep', wi)
        wload(wga[wi], w_in[:, 2816 + 128 * m:2816 + 128 * (m + 1)].rearrange("(k p) n -> p k n", p=128), wk_)
        wload(wgb[wi], w_in[:, 3840 + 128 * m:3840 + 128 * (m + 1)].rearrange("(k p) n -> p k n", p=128), wk_)
        wload(wba[wi], w_ba[:, 128 * m:128 * (m + 1)].rearrange("(k p) n -> p k n", p=128), wk_)
        wload(wbb[wi], w_bb[:, 128 * m:128 * (m + 1)].rearrange("(k p) n -> p k n", p=64), wk_)
        for (o, n) in BLKS:
            MMS([(PS(0)[:, 0:n], [(wga[wi][:, k, :], xnB[:, k, o:o + n]) for k in range(8)])], [wk_], [pk(0)])
            ACT(sga[:, 0:n], PS(0)[:, 0:n], AF.Sigmoid, [pk(0), 'gateB'], ['sga'], bias=gateB[:, m:m + 1])
            MMS([(PS(1)[:, 0:n], [(wba[wi][:, k, :], yT[:, k, o:o + n]) for k in range(4)])], [wk_], [pk(1)])
            TT('vector', ta[:, 0:n], PS(1)[:, 0:n], sga[:, 0:n], ALU.mult, [pk(1), 'sga'], ['ta'])
            MMS([(PS(2)[:, 0:n], [(wgb[wi][:, k, :], xnB[:, k, o:o + n]) for k in range(8)])], [wk_], [pk(2)])
            ACT(sgb[:, 0:n], PS(2)[:, 0:n], AF.Sigmoid, [pk(2), 'gateB'], ['sgb'], bias=gateB[:, 8 + m:9 + m])
            MMS([(PS(3)[:, 0:n], [(wbb[wi][:, h_, :], OT[:, h_, o:o + n]) for h_ in range(4)])], [wk_], [pk(3)])
            TT('vector', tbb_[:, 0:n], PS(3)[:, 0:n], sgb[:, 0:n], ALU.mult, [pk(3), 'sgb'], ['tb2'])
            TT('vector', mixT[:, m, o:o + n], ta[:, 0:n], tbb_[:, 0:n], ALU.add, ['ta', 'tb2'], ['mixT'])
    S.barrier()
    A.release(b_mark0)
    A.release_top(topA)
    if 'mix' in dbg_d:
        mf = A.alloc([8, OWN17], F32)
        CP('vector', mf, mixT, [], ['mf'])
        dbg_out('mix', mf, 'mf')
        S.barrier()
        A.release(b_mark0)
    if stop_after == 'mix':
        return finish(nc, es, S)

    topB = A.mark_top()
    h_all = A.alloc_top([16, D], F32)
    u2T = A.alloc_top([8, OWN], BF16)
    u2h = A.alloc_top([8, 2], BF16)
    carry = A.alloc_top([NF, 2], F32)
    cw = A.alloc_top([NF, 3], F32); cb = A.alloc_top([NF], F32)
    gffn = A.alloc_top([8], F32); gple = A.alloc_top([8], F32)
    for dst, src, key in ((gffn, g_ffn, 'gffn'), (gple, g_ple, 'gple'), (cw, conv_w, 'cw'), (cb, conv_b, 'cb')):
        S.dma('sync', dst, src, writes=[key])
    b1_mark = A.mark()
    wo = A.alloc([8, D], BF16)
    wload(wo, w_out.rearrange("(k p) n -> p k n", p=128), 'wo')
    xtl = [A.alloc([D], F32) for _ in range(2)]
    hhalo = A.alloc([D], F32)
    uhalo = A.alloc([8, 128], BF16)
    nscr2 = [alloc_nscr('n2a'), alloc_nscr('n2b', share_sq=True)]
    def b1_mm(tl):
        tok = tl * 128
        bi = tl % 2
        for half in range(2):
            MMS([(PS(2 * bi + half), [(mixT[:, k, tok:tok + 128], wo[:, k, half * 512:(half + 1) * 512]) for k in range(8)])],
                ['wo'], [pk(2 * bi + half)])
        S.dma('sync', xtl[bi], xe[6144 + tok:6144 + tok + 128, :], writes=[('xtl', bi)])

    def b1_rest(tl):
        bi = tl % 2
        xk_ = ('xtl', bi)
        hdst = hhalo if tl == 0 else h_all[:, tl - 1, :]
        hk = ('h', tl)
        for half in range(2):
            TT('vector', hdst[:, half * 512:(half + 1) * 512], PS(2 * bi + half), xtl[bi][:, half * 512:(half + 1) * 512], ALU.add,
               [pk(2 * bi + half), xk_], [hk])
        if tl == 0:
            norm_T(hdst, hk, uhalo, 'uhalo', gffn, 'gffn', nscr2[0], 4)
            CP('vector', u2h, uhalo[:, :, 126:128], ['uhalo'], ['u2h'])
        else:
            norm_T(hdst, hk, u2T[:, :, (tl - 1) * 128:tl * 128], ('u2T', (tl - 1) // 2), gffn, 'gffn', nscr2[tl % 2], 4 + tl % 2)

    b1_mm(0)
    for tl in range(17):
        if tl + 1 < 17:
            b1_mm(tl + 1)
        b1_rest(tl)
    S.barrier()
    A.release(a_mark)
    NWB = 4
    wgB = [A.alloc([8, 256], BF16) for _ in range(NWB)]
    wuB = [A.alloc([8, 256], BF16) for _ in range(NWB)]
    wdB = [A.alloc([2, D], BF16) for _ in range(NWB)]
    NG_ = 4
    Gs = [A.alloc([258], F32) for _ in range(2)] * 2
    c1 = [A.alloc([256], F32) for _ in range(2)] * 2
    gl_ = [A.alloc([256], F32) for _ in range(2)] * 2
    AT = [A.alloc([256], BF16) for _ in range(NG_)]
    gcnt = [0]

    def ffn_load(fp):
        wi = fp % NWB
        wkB = ('wB', wi)
        wload(wgB[wi], w_gate[:, 256 * fp:256 * (fp + 1)].rearrange("(k p) n -> p k n", p=128), wkB)
        wload(wuB[wi], w_up[:, 256 * fp:256 * (fp + 1)].rearrange("(k p) n -> p k n", p=128), wkB)
        wload(wdB[wi], w_down[256 * fp:256 * (fp + 1), :].rearrange("(j p) n -> p j n", p=128), wkB)

    def ffn_gu(st):
        fp, u, par = st['fp'], st['u'], st['par']
        wi = fp % NWB
        wkB = ('wB', wi)
        tok = 256 * u
        Gb = par
        Ub = 2 + par
        uk = ('u2T', u)
        MMS([(PS(Gb)[:, 256 * j:256 * (j + 1)], [(wgB[wi][:, k, 128 * j:128 * (j + 1)], u2T[:, k, tok:tok + 256]) for k in range(8)])
             for j in range(2)], [wkB, uk], [pk(Gb)])
        MMS([(PS(Ub)[:, 256 * j:256 * (j + 1)], [(wuB[wi][:, k, 128 * j:128 * (j + 1)], u2T[:, k, tok:tok + 256]) for k in range(8)])
             for j in range(2)], [wkB, uk], [pk(Ub)])

    def ffn_ew(st):
        fp, u, par = st['fp'], st['u'], st['par']
        Gb = par
        Ub = 2 + par
        ats = []
        for j in range(2):
            f = 2 * fp + j
            gi_ = gcnt[0] % NG_
            gcnt[0] += 1
            gk = ('Gs', gi_ % 2)
            CP('scalar', Gs[gi_][:, 0:2], carry[:, f, :], ['carry'], [gk])
            CP('scalar', Gs[gi_][:, 2:258], PS(Gb)[:, 256 * j:256 * (j + 1)], [pk(Gb)], [gk])
            ck_ = ('c1', gi_ % 2)
            ACT(c1[gi_], Gs[gi_][:, 0:256], AF.Copy, [gk, 'cw'], [ck_], scale=cw[:, f, 0:1])
            STT('vector', c1[gi_], Gs[gi_][:, 1:257], cw[:, f, 1:2], c1[gi_], ALU.mult, ALU.add, [gk, 'cw', ck_], [ck_])
            STT('vector', c1[gi_], Gs[gi_][:, 2:258], cw[:, f, 2:3], c1[gi_], ALU.mult, ALU.add, [gk, 'cw', ck_], [ck_])
            CP('scalar', carry[:, f, :], Gs[gi_][:, 256:258], [gk], ['carry'])
            ACT(gl_[gi_], c1[gi_], AF.Gelu_apprx_tanh, [ck_, 'cb'], [('gl', gi_ % 2)], bias=cb[:, f:f + 1])
            TT('vector', AT[gi_], gl_[gi_], PS(Ub)[:, 256 * j:256 * (j + 1)], ALU.mult, [('gl', gi_ % 2), pk(Ub)], [('AT', gi_)])
            ats.append(gi_)
        return ats

    def ffn_down(st, ats):
        fp = st['fp']
        wi = fp % NWB
        wkB = ('wB', wi)
        steps = []
        for t2 in range(2):
            for half in range(2):
                for j in range(2):
                    steps.append((PS(4 + 2 * t2 + half), AT[ats[j]][:, 128 * t2:128 * (t2 + 1)],
                                  wdB[wi][:, j, 512 * half:512 * (half + 1)], st['first'] and j == 0, st['last'] and j == 1))
        MMRAW(steps, [('AT', ats[0]), ('AT', ats[1]), wkB], [pk(4), pk(5), pk(6), pk(7)])

    def ffn_evac(st):
        u = st['u']
        for t2 in range(2):
            hk = ('h', 1 + 2 * u + t2)
            for half in range(2):
                hsl = h_all[:, 2 * u + t2, 512 * half:512 * (half + 1)]
                TT('vector', hsl, hsl, PS(4 + 2 * t2 + half), ALU.add, [pk(4 + 2 * t2 + half), hk], [hk])

    def ffn_halo(fp):
        wi = fp % NWB
        wkB = ('wB', wi)
        MMS([(PS(4)[:, 2 * j:2 * j + 2], [(wgB[wi][:, k, 128 * j:128 * (j + 1)], u2h[:, k, :]) for k in range(8)])
             for j in range(2)], [wkB, 'u2h'], [pk(4)])
        CP('vector', carry[:, 2 * fp:2 * fp + 2, :], PS(4)[:, 0:4].rearrange("p (j t) -> p j t", j=2), [pk(4)], ['carry'])

    NFP = NF // 2
    quads = [tuple(range(a, min(a + 2, NFP))) for a in range(0, NFP, 2)]
    seq = []
    for qi, q in enumerate(quads):
        for u in range(8):
            for idx, fp in enumerate(q):
                seq.append({'fp': fp, 'u': u, 'first': idx == 0, 'last': idx == len(q) - 1, 'qi': qi, 'par': len(seq) % 2})
    NS = len(seq)
    for fp in quads[0]:
        ffn_load(fp)
    for fp in quads[0]:
        ffn_halo(fp)
    ffn_gu(seq[0])
    ffn_gu(seq[1])
    ats_q = {0: ffn_ew(seq[0])}
    for si in range(NS):
        st = seq[si]
        qi = st['qi']
        if st['u'] == 0 and st['first'] and qi + 1 < len(quads):
            for fp in quads[qi + 1]:
                ffn_load(fp)
        ffn_down(st, ats_q.pop(si))
        if si + 2 < NS:
            ffn_gu(seq[si + 2])
        if si + 1 < NS:
            ats_q[si + 1] = ffn_ew(seq[si + 1])
        if st['last']:
            ffn_evac(st)
            if st['u'] == 5 and qi + 1 < len(quads):
                for fp in quads[qi + 1]:
                    ffn_halo(fp)
    S.barrier()
    A.release(a_mark)
    gfin = A.alloc([D], F32)
    S.dma('sync', gfin, g_fin, writes=['gfin'])
    pwg = A.alloc([8, D], BF16)
    pwp = A.alloc([2, D], BF16)
    wload(pwg, ple_wg.rearrange("(k p) n -> p k n", p=128), 'pwg')
    wload(pwp, ple_wp.rearrange("(k p) n -> p k n", p=128), 'pwp')
    NB3 = 4
    nscr3 = [alloc_nscr('n3_%d' % i, share_sq=(i > 0)) for i in range(NB3)]
    u3T = [A.alloc([8, 128], BF16) for _ in range(NB3)]
    ptl = [A.alloc([256], F32) for _ in range(NB3)]
    ptb = [A.alloc([256], BF16) for _ in range(NB3)]
    pT = [A.alloc([2, 128], BF16) for _ in range(NB3)]
    sgp = [A.alloc([D], F32) for _ in range(NB3)]; tpl = [A.alloc([D], F32) for _ in range(NB3)]
    st3 = [A.alloc([4], F32) for _ in range(NB3)]; junk = A.alloc([D], BF16)
    otl = [A.alloc([D], F32) for _ in range(NB3)]
    def b3_a(tl):
        bi = tl % NB3
        pbi = tl % 2
        orow = tl * 128
        hk = ('h', tl)
        ht = h_all[:, tl, :]
        norm_T(ht, hk, u3T[bi], ('u3T', bi), gple, 'gple', nscr3[bi], 4 + pbi)
        S.dma('sync', ptl[bi], pe[orow:orow + 128, :], writes=[('ptl', bi)])
        CP('scalar', ptb[bi], ptl[bi], [('ptl', bi)], [('ptb', bi)])
        TRS([(PSB(6 + pbi)[:, kk * 128:(kk + 1) * 128], ptb[bi][:, kk * 128:(kk + 1) * 128], 128) for kk in range(2)],
            [('ptb', bi)], [pk(6 + pbi)])
        CP('scalar', pT[bi].rearrange("p k n -> p (k n)"), PSB(6 + pbi)[:, 0:256], [pk(6 + pbi)], [('pT', bi)])

    def b3_b(tl):
        bi = tl % NB3
        pbi = tl % 2
        orow = tl * 128
        hk = ('h', tl)
        ht = h_all[:, tl, :]
        for half in range(2):
            hs_ = slice(half * 512, (half + 1) * 512)
            MMS([(PS(half), [(u3T[bi][:, k, :], pwg[:, k, hs_]) for k in range(8)])], [('u3T', bi), 'pwg'], [pk(half)])
            ACT(sgp[bi][:, hs_], PS(half), AF.Sigmoid, [pk(half)], [('sgp', bi)])
            MMS([(PS(2 + half), [(pT[bi][:, k, :], pwp[:, k, hs_]) for k in range(2)])], [('pT', bi), 'pwp'], [pk(2 + half)])
            TT('vector', tpl[bi][:, hs_], PS(2 + half), sgp[bi][:, hs_], ALU.mult, [pk(2 + half), ('sgp', bi)], [('tpl', bi)])
        TT('vector', ht, ht, tpl[bi], ALU.add, [('tpl', bi), hk], [hk])

    def b3_c(tl):
        bi = tl % NB3
        orow = tl * 128
        hk = ('h', tl)
        ht = h_all[:, tl, :]
        sk_ = ('st3', bi)
        ACT(junk, ht, AF.Square, [hk], [sk_], accum=st3[bi][:, 0:1])
        ACT(st3[bi][:, 2:3], st3[bi][:, 0:1], AF.Ln, [sk_, 'eps_c'], [sk_], bias=eps_c[:, 0:1], scale=1.0 / D)
        ACT(st3[bi][:, 3:4], st3[bi][:, 2:3], AF.Exp, [sk_], [sk_], scale=-0.5)
        ok_ = ('otl', bi)
        STT('vector', otl[bi], ht, st3[bi][:, 3:4], gfin, ALU.mult, ALU.mult, [hk, sk_, 'gfin'], [ok_])
        S.dma('sync', out_d[orow:orow + 128, :], otl[bi], reads=[ok_])

    b3_a(0)
    b3_a(1)
    for tl in range(16):
        if tl + 2 < 16:
            b3_a(tl + 2)
        b3_b(tl)
        if tl >= 2:
            b3_c(tl - 2)
    b3_c(14)
    b3_c(15)
    return finish(nc, es, S)


def finish(nc, es, S):
    S.barrier()
    zf = getattr(S, 'zero_psum', None)
    if zf is not None:
        zf()
        S.barrier()
    block = es.enter_context(nc.Block())
    S.emit(block)
    es.close()
    return nc


def host_prep(inputs):
    f32 = np.float32
    x = np.asarray(inputs['x'], f32)
    p = np.asarray(inputs['p'], f32)[0]
    sq = lambda k: np.asarray(inputs[k], f32)[0]
    fm = lambda v, k: np.ascontiguousarray(v.reshape(k, 128).T)
    common = {}
    common['ident'] = np.eye(128, dtype=f32)
    bb, aa = np.meshgrid(np.arange(128), np.arange(128), indexing='ij')
    cm = np.zeros((128, 3, 256), f32)
    cm[:, 0, :128] = (bb >= aa)
    cm[:, 1, :128] = (bb <= aa)
    s_idx = np.arange(128) // 16
    cm[:, 2, :128] = (s_idx[None, :] >= s_idx[:, None])
    cm[:, 2, 128:] = 1.0
    common['cmask'] = cm
    pm = np.zeros((128, 128), f32)
    for m_ in range(128):
        d_ = m_ % 64
        src = m_ + 8 if d_ < 8 else (m_ - 8 if d_ < 16 else m_)
        pm[src, m_] = 1.0
    common['permT'] = pm
    sh = np.zeros((128, 64), f32)
    sh[64 + np.arange(64), np.arange(64)] = 1.0
    common['shiftD'] = sh
    common['kp'] = np.broadcast_to(np.array(KP, f32)[None, :], (64, NKP)).copy()
    common['w_in'] = sq('w_in')
    common['g_mix'] = fm(sq('mix_norm_g'), 8)
    common['gate_b'] = fm(sq('gate_b'), 16)
    common['lam_re'] = np.ascontiguousarray(sq('ssm_lam_re').T)
    common['lam_im'] = np.ascontiguousarray(sq('ssm_lam_im').T)
    common['log_dt'] = np.broadcast_to(sq('ssm_log_dt')[None, :], (64, 32)).copy()
    common['b_re'] = np.ascontiguousarray(sq('ssm_b_re').transpose(1, 0, 2))
    common['b_im'] = np.ascontiguousarray(sq('ssm_b_im').transpose(1, 0, 2))
    common['c_re'] = np.ascontiguousarray(sq('ssm_c_re').transpose(2, 0, 1))
    common['c_im'] = np.ascontiguousarray(sq('ssm_c_im').transpose(2, 0, 1))
    common['d_col'] = np.ascontiguousarray(np.tile(sq('ssm_d').T, (8, 1)))
    common['glu_w'] = sq('ssm_glu_w')
    common['glu_b'] = fm(sq('ssm_glu_b'), 4)
    common['w_ba'] = sq('w_branch_a')
    common['w_bb'] = sq('w_branch_b')
    common['w_out'] = sq('w_out')
    common['g_ffn'] = fm(sq('ffn_norm_g'), 8)
    common['w_gate'] = sq('ffn_w_gate')
    common['w_up'] = sq('ffn_w_up')
    common['conv_w'] = np.ascontiguousarray(sq('ffn_conv_w').reshape(3, NF, 128).transpose(2, 1, 0))
    common['conv_b'] = fm(sq('ffn_conv_b'), NF)
    common['w_down'] = sq('ffn_w_down')
    common['g_ple'] = fm(sq('ple_norm_g'), 8)
    common['ple_wg'] = sq('ple_w_gate')
    common['ple_wp'] = sq('ple_w_proj')
    common['g_fin'] = np.broadcast_to(np.asarray(inputs['final_norm_g'], f32)[None, :], (128, D)).copy()
    half = 8
    freqs = (np.float32(500000.0) ** (-np.arange(half, dtype=f32) * np.float32(2.0 / 16))).astype(f32)
    maps = []
    for core in range(8):
        b, q = core // 4, core % 4
        s = q * OWN
        t0 = s - (EXT - OWN)
        m = dict(common)
        xe_ = np.zeros((EXT, D), f32)
        lo = max(t0, 0)
        xe_[lo - t0:] = x[b, lo:s + OWN]
        m['xe'] = xe_
        m['pe'] = np.ascontiguousarray(p[b, s:s + OWN])
        tokn = (s - 128 - 2048) + np.arange(OWN17 + 2048)
        posn = np.maximum(tokn, 0).astype(f32)
        angn = posn[:, None] * freqs[None, :]
        cn = np.cos(angn).astype(f32).T
        sn_ = np.sin(angn).astype(f32).T
        rotn = np.zeros((96, 2, OWN17 + 2048), f32)
        rotn[:, 0] = 1.0
        for hb in (0, 64):
            rotn[hb:hb + 8, 0] = cn; rotn[hb + 8:hb + 16, 0] = cn
            rotn[hb:hb + 8, 1] = -sn_; rotn[hb + 8:hb + 16, 1] = sn_
        m['rotn'] = rotn
        for g, dil in enumerate(DILS):
            nk = OWN17 + 128 * dil
            base = s - 128 - 128 * dil
            per = nk // dil
            r_ = np.arange(nk) // per
            i_ = np.arange(nk) % per
            tok = base + r_ + dil * i_
            v = (tok >= 0).astype(f32)
            nblk = (per + 127) // 128
            vv = np.zeros((128, 64), f32)
            for r in range(dil):
                for j in range(nblk):
                    seg = v[r * per + j * 128: min(r * per + (j + 1) * 128, (r + 1) * per)]
                    vv[:len(seg), j * dil + r] = seg
            m['valid%d' % g] = vv
        maps.append(m)
    return maps


_CACHE = {}


def kernel(**inputs):
    maps = host_prep(inputs)
    if 'nc' not in _CACHE:
        nc = bass.Bass("TRN2", target_bir_lowering=False)
        build_full(nc)
        _CACHE['nc'] = nc
    nc = _CACHE['nc']
    res = run_bass_kernel_spmd(nc, maps, core_ids=list(range(8)))
    out = np.zeros((2, SEQ, D), np.float32)
    for core in range(8):
        b, q = core // 4, core % 4
        out[b, q * OWN:(q + 1) * OWN] = res.results[core]["out"]
    return out
```

```python
import math
from contextlib import ExitStack
import numpy as np
import ml_dtypes
import concourse.bass as bass
import concourse.mybir as mybir
from concourse.bass_utils import run_bass_kernel_spmd

F32 = mybir.dt.float32
BF16 = mybir.dt.bfloat16
I32 = mybir.dt.int32
ALU = mybir.AluOpType
AF = mybir.ActivationFunctionType

D = 1024
SEQ = 8192
OWN = 2048
OWN17 = 2176
EXT = 8320
NT_EXT = 65
DFF = 2816
NF = 22
PI = math.pi
TWO_PI = 2 * math.pi
DILS = (1, 4, 16)
KP = ([-1, -2, -3, -4, -5, -6, -7, -8] + list(range(1, 17))
      + [112, 96, 80, 64, 48, 32, 16, 0]
      + [128 * (15 - k) for k in range(16)]
      + [2048])
NKP = len(KP)
EPS = 1e-6


class Sched:
    ENGS = ['sync', 'scalar', 'vector', 'gpsimd', 'tensor']

    def __init__(self, nc, es, nds=8):
        self.nc = nc
        self.sem = {e: es.enter_context(nc.semaphore('s_' + e)) for e in self.ENGS}
        self.cnt = {e: 0 for e in self.ENGS}
        self.prog = {e: [] for e in self.ENGS}
        self.seen = {e: {} for e in self.ENGS}
        self.lastw = {}
        self.readers = {}
        self.nds = nds
        self.dsem = {}
        self.dval = {}
        self.dnext = {}
        for q in ('sync', 'scalar', 'gpsimd'):
            self.dsem[q] = [es.enter_context(nc.semaphore('d_%s%d' % (q, i))) for i in range(nds)]
            self.dval[q] = [0] * nds
            self.dnext[q] = 0

    def semh(self, sk):
        if isinstance(sk, str):
            return self.sem[sk]
        return self.dsem[sk[0]][sk[1]]

    def _waits(self, eng, reads, writes):
        need = {}

        def add(tok):
            sk, v = tok
            if need.get(sk, 0) < v:
                need[sk] = v
        for r in reads:
            if r in self.lastw:
                add(self.lastw[r])
        for w in writes:
            if w in self.lastw:
                add(self.lastw[w])
            for sk, v in self.readers.get(w, {}).items():
                add((sk, v))
        out = []
        for sk, v in need.items():
            if self.seen[eng].get(sk, 0) >= v:
                continue
            if sk == eng and eng == 'tensor':
                continue
            self.seen[eng][sk] = v
            out.append((sk, v))
        return out

    def _record(self, tok, reads, writes):
        for r in reads:
            d = self.readers.setdefault(r, {})
            if d.get(tok[0], 0) < tok[1]:
                d[tok[0]] = tok[1]
        for w in writes:
            self.lastw[w] = tok
            self.readers[w] = {}

    def op(self, eng, fn, reads=(), writes=()):
        waits = self._waits(eng, reads, writes)
        self.cnt[eng] += 1
        tok = (eng, self.cnt[eng])
        self.prog[eng].append((waits, fn, (self.sem[eng], 1)))
        self._record(tok, reads, writes)

    def dma(self, q, out, in_, reads=(), writes=(), **kw):
        waits = self._waits(q, reads, writes)
        i = self.dnext[q]
        self.dnext[q] = (i + 1) % self.nds
        sk = (q, i)
        if self.dval[q][i] > 0 and self.seen[q].get(sk, 0) < self.dval[q][i]:
            waits.append((sk, self.dval[q][i]))
            self.seen[q][sk] = self.dval[q][i]
        self.dval[q][i] += 16
        tok = (sk, self.dval[q][i])
        self.prog[q].append((waits, (lambda e: e.dma_start(out=out, in_=in_, **kw)), (self.dsem[q][i], 16)))
        self._record(tok, reads, writes)

    def barrier(self):
        for e in self.ENGS:
            waits = []
            for o in self.ENGS:
                if o != e and self.cnt[o] > 0 and self.seen[e].get(o, 0) < self.cnt[o]:
                    waits.append((o, self.cnt[o]))
                    self.seen[e][o] = self.cnt[o]
            for q in self.dsem:
                for i in range(self.nds):
                    sk = (q, i)
                    if self.dval[q][i] > 0 and self.seen[e].get(sk, 0) < self.dval[q][i]:
                        waits.append((sk, self.dval[q][i]))
                        self.seen[e][sk] = self.dval[q][i]
            if e != 'tensor' and self.cnt[e] > 0 and self.seen[e].get(e, 0) < self.cnt[e]:
                waits.append((e, self.cnt[e]))
                self.seen[e][e] = self.cnt[e]
            if waits:
                self.prog[e].append((waits, None, None))
        self.lastw = {}
        self.readers = {}

    def emit(self, block):
        for e in self.ENGS:
            def mk(e):
                def f(eng):
                    for waits, fn, inc in self.prog[e]:
                        for sk, v in waits:
                            eng.wait_ge(self.semh(sk), v)
                        if fn is not None:
                            ins = fn(eng)
                            ins.then_inc(inc[0], inc[1])
                return f
            getattr(block, e)(mk(e))


class Arena:
    def __init__(self, t, cap_bytes):
        self.t = t
        self.cap = cap_bytes
        self.off = 0
        self.top = cap_bytes
        self.peak = 0

    def mark(self):
        return self.off

    def release(self, m):
        self.off = m

    def mark_top(self):
        return self.top

    def release_top(self, m):
        self.top = m

    def _view(self, off, nb, shape, dt, parts):
        a = self.t[0:parts, off // 2:(off + nb) // 2]
        if dt != BF16:
            a = a.bitcast(dt)
        if len(shape) == 1:
            return a
        names = ' '.join('d%d' % i for i in range(len(shape)))
        kw = {'d%d' % i: shape[i] for i in range(1, len(shape))}
        return a.rearrange('p (%s) -> p %s' % (names, names), **kw)

    @staticmethod
    def _nb(shape, dt):
        n = 1
        for s in shape:
            n *= s
        return n * (2 if dt == BF16 else 4)

    def alloc(self, shape, dt=BF16, parts=128):
        off = (self.off + 31) // 32 * 32
        nb = self._nb(shape, dt)
        assert off + nb <= self.top, ("arena overflow", off, nb, self.top)
        self.off = off + nb
        self.peak = max(self.peak, self.off + (self.cap - self.top))
        return self._view(off, nb, shape, dt, parts)

    def alloc_top(self, shape, dt=BF16, parts=128):
        nb = self._nb(shape, dt)
        off = (self.top - nb) // 32 * 32
        assert off >= self.off, ("arena overflow (top)", off, nb, self.off)
        self.top = off
        self.peak = max(self.peak, self.off + (self.cap - self.top))
        return self._view(off, nb, shape, dt, parts)


def build_full(nc, dbg=None, stop_after=None, att_limit=(4, 3)):
    es = ExitStack()
    E = es.enter_context

    def din(name, shape, dt=F32):
        return nc.dram_tensor(name, list(shape), dt, kind="ExternalInput").ap()
    xe = din("xe", [EXT, D])
    pe = din("pe", [OWN, 256])
    ident_d = din("ident", [128, 128])
    cmask_d = din("cmask", [128, 3, 256])
    perm_d = din("permT", [128, 128])
    shift_d = din("shiftD", [128, 64])
    kp_d = din("kp", [64, NKP])
    rot_n = din("rotn", [96, 2, OWN17 + 2048])
    valid_d = [din("valid%d" % g, [128, 64]) for g in range(3)]
    w_in = din("w_in", [D, 4864])
    g_mix = din("g_mix", [128, 8])
    gate_b = din("gate_b", [128, 16])
    lam_re = din("lam_re", [64, 32])
    lam_im = din("lam_im", [64, 32])
    log_dt = din("log_dt", [64, 32])
    b_re = din("b_re", [64, 32, 16])
    b_im = din("b_im", [64, 32, 16])
    c_re = din("c_re", [64, 32, 16])
    c_im = din("c_im", [64, 32, 16])
    d_col = din("d_col", [128, 32])
    glu_w = din("glu_w", [512, 512])
    glu_b = din("glu_b", [128, 4])
    w_ba = din("w_ba", [512, D])
    w_bb = din("w_bb", [256, D])
    w_out = din("w_out", [D, D])
    g_ffn = din("g_ffn", [128, 8])
    w_gate = din("w_gate", [D, DFF])
    w_up = din("w_up", [D, DFF])
    conv_w = din("conv_w", [128, NF, 3])
    conv_b = din("conv_b", [128, NF])
    w_down = din("w_down", [DFF, D])
    g_ple = din("g_ple", [128, 8])
    ple_wg = din("ple_wg", [D, D])
    ple_wp = din("ple_wp", [256, D])
    g_fin = din("g_fin", [128, D])
    out_d = nc.dram_tensor("out", [OWN, D], F32, kind="ExternalOutput").ap()
    dbg = dbg or {}
    dbg_d = {k: nc.dram_tensor("dbg_" + k, list(shp), F32, kind="ExternalOutput").ap() for k, shp in dbg.items()}

    CAP = 207 * 1024
    arena_t = E(nc.sbuf_tensor("arena", [128, CAP // 2], BF16))
    A = Arena(arena_t, CAP)
    ps = [E(nc.psum_tensor("ps%d" % i, [128, 512], F32)) for i in range(8)]
    S = Sched(nc, es)

    def PS(i, parts=128):
        return ps[i][0:parts, :]

    def PSB(i, parts=128):
        return ps[i][0:parts, :].bitcast(BF16)

    def pk(i):
        return ('ps', i)

    def TT(eng, out, in0, in1, op, r, w):
        S.op(eng, lambda e: e.tensor_tensor(out=out, in0=in0, in1=in1, op=op), r, w)

    def TS(eng, out, in0, s1, s2, op0, op1, r, w):
        if op1 is None:
            S.op(eng, lambda e: e.tensor_scalar(out=out, in0=in0, scalar1=s1, scalar2=None, op0=op0), r, w)
        else:
            S.op(eng, lambda e: e.tensor_scalar(out=out, in0=in0, scalar1=s1, scalar2=s2, op0=op0, op1=op1), r, w)

    def STT(eng, out, in0, scalar, in1, op0, op1, r, w):
        eng = 'vector'
        S.op(eng, lambda e: e.scalar_tensor_tensor(out=out, in0=in0, scalar=scalar, in1=in1, op0=op0, op1=op1), r, w)

    def ACT(out, in_, func, r, w, bias=None, scale=None, accum=None):
        kw = {}
        if bias is not None:
            kw['bias'] = bias
        if scale is not None:
            kw['scale'] = scale
        if accum is not None:
            kw['accum_out'] = accum
        S.op('scalar', lambda e: e.activation(out=out, in_=in_, func=func, **kw), r, w)

    def CP(eng, out, in_, r, w):
        if eng == 'scalar':
            ACT(out, in_, AF.Copy, r, w)
        else:
            S.op(eng, lambda e: e.tensor_copy(out=out, in_=in_), r, w)

    def MMS(groups, r, w):
        def f(e):
            ins = None
            for out, pairs in groups:
                n = len(pairs)
                for i, (l, rh) in enumerate(pairs):
                    ins = e.matmul(out, lhsT=l, rhs=rh, start=(i == 0), stop=(i == n - 1))
            return ins
        S.op('tensor', f, r, w)

    def MMRAW(steps, r, w):
        def f(e):
            ins = None
            for out, l, rh, st, sp in steps:
                ins = e.matmul(out, lhsT=l, rhs=rh, start=st, stop=sp)
            return ins
        S.op('tensor', f, r, w)

    def TRS(items, r, w):
        def f(e):
            ins = None
            for out, in_, npart in items:
                ins = e.transpose(out=out, in_=in_, identity=ident[0:npart, 0:npart])
            return ins
        S.op('tensor', f, list(r) + ['ident'], w)

    def _zero_psum():
        for i_ in range(8):
            S.op('vector', lambda e, i_=i_: e.memset(ps[i_][:, :], 0.0), [], [pk(i_)])
    S.zero_psum = _zero_psum
    flip = [0]

    def evac_eng():
        flip[0] ^= 1
        return 'vector' if flip[0] else 'scalar'

    def wload(dst, src, key):
        S.dma('gpsimd', dst, src, writes=[key])

    def dbg_out(name, ap_f32, key):
        if name in dbg_d:
            S.dma('sync', dbg_d[name], ap_f32, reads=[key])

    ident_f = A.alloc_top([128], F32)
    ident = A.alloc_top([128], BF16)
    cmask = A.alloc_top([3, 256], BF16)
    permT = A.alloc_top([128], BF16)
    shiftD = A.alloc_top([64], F32)
    S.dma('sync', shiftD, shift_d, writes=['shiftD'])
    gm = A.alloc_top([8], F32)
    eps_c = A.alloc_top([1], F32)
    S.op('gpsimd', lambda e: e.memset(eps_c, EPS), [], ['eps_c'])
    m0 = A.mark()
    cmask_f = A.alloc([3, 256], F32)
    perm_f = A.alloc([128], F32)
    S.dma('sync', perm_f, perm_d, writes=['perm_f'])
    CP('vector', permT, perm_f, ['perm_f'], ['permT'])
    S.dma('sync', ident_f, ident_d, writes=['ident_f'])
    S.dma('sync', cmask_f, cmask_d, writes=['cmask_f'])
    S.dma('sync', gm, g_mix, writes=['gm'])
    CP('vector', ident, ident_f, ['ident_f'], ['ident'])
    CP('vector', cmask, cmask_f, ['cmask_f'], ['cmask'])
    S.barrier()
    A.release(m0)

    def norm_stats(src, src_key, scr):
        sq, xb, st, k = scr['sq'], scr['xb'], scr['st'], scr['key']
        ACT(sq, src, AF.Square, [src_key], [k + 'sq', k + 'st'], accum=st[:, 0:1])
        ACT(st[:, 2:3], st[:, 0:1], AF.Ln, [k + 'st', 'eps_c'], [k + 'st'], bias=eps_c[:, 0:1], scale=1.0 / D)
        ACT(st[:, 3:4], st[:, 2:3], AF.Exp, [k + 'st'], [k + 'st'], scale=-0.5)
        ACT(xb, src, AF.Copy, [src_key, k + 'st'], [k + 'xb'], scale=st[:, 3:4])

    def norm_tr(dstT, dst_key, gfm, gkey, scr, psb):
        xb, k = scr['xb'], scr['key']
        TRS([(PSB(psb)[:, kk * 128:(kk + 1) * 128], xb[:, kk * 128:(kk + 1) * 128], 128) for kk in range(8)],
            [k + 'xb'], [pk(psb)])
        TT('vector', dstT, PSB(psb).rearrange("p (k n) -> p k n", k=8), gfm.unsqueeze(2).to_broadcast([128, 8, 128]),
           ALU.mult, [pk(psb), gkey], [dst_key])

    def norm_T(src, src_key, dstT, dst_key, gfm, gkey, scr, psb):
        norm_stats(src, src_key, scr)
        norm_tr(dstT, dst_key, gfm, gkey, scr, psb)

    junk_sq = []

    def alloc_nscr(key, share_sq=False):
        if not (share_sq and junk_sq):
            junk_sq.append(A.alloc([D], BF16))
        return {'sq': junk_sq[-1], 'xb': A.alloc([D], BF16), 'st': A.alloc([4], F32), 'key': key}

    topA = A.mark_top()
    xnA = A.alloc_top([8, OWN], BF16)
    xnB = A.alloc_top([8, OWN17], BF16)
    a_mark = A.mark()

    lr = A.alloc([32], F32, 64); li = A.alloc([32], F32, 64); ldt = A.alloc([32], F32, 64)
    kp = A.alloc([NKP], F32, 64)
    PWr = A.alloc([32, NKP], F32, 64); PWi = A.alloc([32, NKP], F32, 64)
    Cr = A.alloc([32, 16], F32, 64); Ci = A.alloc([32, 16], F32, 64)
    Bbr = A.alloc([32, 16], F32, 64); Bbi = A.alloc([32, 16], F32, 64)
    Aa = A.alloc([32, 2], F32, 64); Ab = A.alloc([32, 2], F32, 64)
    A2tab = A.alloc([32, 8], F32, 64)
    dcol = A.alloc([32], F32)
    UT = A.alloc([32, 2, 136], BF16)
    un_ = A.alloc([32 * 2 * 136], BF16)
    SPb = un_[0:64, :].rearrange("p (g r c) -> p g r c", g=32, r=2)
    FT = un_[:, 0:32 * 2 * 2 * 64].rearrange("p (g b r n) -> p g b r n", g=32, b=2, r=2)
    y_mark = A.mark()
    TG = 'vector'
    nscr_l = [alloc_nscr('n1a'), alloc_nscr('n1b', share_sq=True)]
    xt_buf = [A.alloc([D], F32) for _ in range(2)]

    def piece_cfg(pc):
        ntile = 17 if pc == 3 else 16
        xn = xnB if pc in (1, 3) else xnA
        xk = 'xnB' if pc in (1, 3) else 'xnA'
        return ntile, xn, xk

    def norm_tile_a(pc, t):
        ntile, xn, xk = piece_cfg(pc)
        if t >= ntile:
            return
        et = pc * 16 + t
        xb_ = xt_buf[t % 2]
        xkey = 'xt%d' % (t % 2)
        S.dma('sync', xb_, xe[et * 128:(et + 1) * 128, :], writes=[xkey])
        norm_stats(xb_, xkey, nscr_l[t % 2])

    def norm_tile_b(pc, t):
        ntile, xn, xk = piece_cfg(pc)
        if t < 0 or t >= ntile:
            return
        norm_tr(xn[:, :, t * 128:(t + 1) * 128], xk, gm, 'gm', nscr_l[t % 2], 4 + t % 2)

    def norm_tile(pc, t):
        norm_tile_a(pc, t)
        norm_tile_b(pc, t)

    wssm = A.alloc([8, 512], BF16)
    wload(wssm, w_in[:, 0:512].rearrange("(k p) n -> p k n", p=128), 'wssm')
    Ucm = A.alloc([32, 16, 16], BF16)
    OWN_CTS = [(0, 68), (68, 68)]
    for t in range(16):
        norm_tile(0, t)
    pending_red = []

    def piece_part1(pc):
        ntile, xn, xk = piece_cfg(pc)
        nch = ntile * 8
        cts = OWN_CTS if pc == 3 else [(0, 128)]
        nt_next = 0
        for (c0, ncc) in cts:
            for tau in range(16):
                pb = tau % 2
                lo = c0 * 16 + tau
                MMS([(PS(pb, ncc), [(xn[:, k, lo:lo + 16 * (ncc - 1) + 1:16], wssm[:, k, :]) for k in range(8)])],
                    [xk, 'wssm'], [pk(pb)])
                CP(evac_eng(), Ucm[0:ncc, :, tau, :], PS(pb, ncc).rearrange("p (g h) -> p g h", g=32), [pk(pb)], ['Ucm'])
                if pc < 3 and nt_next < 18:
                    norm_tile_b(pc + 1, nt_next - 1)
                    norm_tile_a(pc + 1, nt_next)
                    nt_next += 1
                if pending_red:
                    pending_red.pop(0)()
            for g in range(32):
                pb = 2 + g % 2
                TRS([(PSB(pb)[:, b * ncc:(b + 1) * ncc], Ucm[0:ncc, g, 8 * b:8 * b + 8, :].rearrange("p s h -> p (s h)"), ncc) for b in range(2)],
                    ['Ucm'], [pk(pb)])
                CP(evac_eng(), UT[:, g, :, c0:c0 + ncc], PSB(pb)[:, 0:2 * ncc].rearrange("p (b c) -> p b c", b=2),
                   [pk(pb)], ['UT'])
        while pc < 3 and nt_next < 18:
            norm_tile_b(pc + 1, nt_next - 1)
            norm_tile_a(pc + 1, nt_next)
            nt_next += 1

    piece_part1(0)
    par_mark = A.mark()
    for dst, src, key in ((lr, lam_re, 'lr'), (li, lam_im, 'li'), (ldt, log_dt, 'ldt'), (kp, kp_d, 'kp'),
                          (Cr, c_re, 'Cr'), (Ci, c_im, 'Ci'), (dcol, d_col, 'dcol')):
        S.dma('sync', dst, src, writes=[key])
    Br = A.alloc([32, 16], F32, 64); Bi = A.alloc([32, 16], F32, 64)
    S.dma('sync', Br, b_re, writes=['Br']); S.dma('sync', Bi, b_im, writes=['Bi'])
    dt_ = A.alloc([32], F32, 64); lrdt = A.alloc([32], F32, 64); lidt = A.alloc([32], F32, 64)
    ACT(dt_, ldt, AF.Exp, ['ldt'], ['dt'])
    TT(TG, lrdt, lr, dt_, ALU.mult, ['lr', 'dt'], ['lrdt'])
    TT(TG, lidt, li, dt_, ALU.mult, ['li', 'dt'], ['lidt'])
    SH = [64, 32, NKP]
    argm = A.alloc([32, NKP], F32, 64); ang = A.alloc([32, NKP], F32, 64)
    tq = A.alloc([32, NKP], F32, 64); ti = A.alloc([32, NKP], I32, 64); tk = argm
    mag = A.alloc([32, NKP], F32, 64); sn = A.alloc([32, NKP], F32, 64); cs = sn
    kpb = kp.unsqueeze(1).to_broadcast(SH)
    TT(TG, argm, lrdt.unsqueeze(2).to_broadcast(SH), kpb, ALU.mult, ['lrdt', 'kp'], ['argm'])
    ACT(mag, argm, AF.Exp, ['argm'], ['mag'])
    TT(TG, ang, lidt.unsqueeze(2).to_broadcast(SH), kpb, ALU.mult, ['lidt', 'kp'], ['ang'])

    def sin_rr(dst, shift, key):
        R_, W_ = ['ang', 'rr'], ['rr', 'argm']
        TS(TG, tq, ang, shift, 1.0 / TWO_PI, ALU.add, ALU.mult, R_, W_)
        CP(TG, ti, tq, R_, W_)
        CP(TG, tk, ti, R_, W_)
        TS(TG, tq, ang, shift, None, ALU.add, None, R_, W_)
        STT(TG, tq, tk, -6.28125, tq, ALU.mult, ALU.add, R_, W_)
        STT(TG, tq, tk, -(TWO_PI - 6.28125), tq, ALU.mult, ALU.add, R_, W_)
        TS(TG, tk, tq, PI, -TWO_PI, ALU.is_gt, ALU.mult, R_, W_)
        TT(TG, tq, tq, tk, ALU.add, R_, W_)
        TS(TG, tk, tq, -PI, TWO_PI, ALU.is_lt, ALU.mult, R_, W_)
        TT(TG, tq, tq, tk, ALU.add, R_, W_)
        TS(TG, tq, tq, PI, -PI, ALU.min, ALU.max, R_, W_)
        ACT(dst, tq, AF.Sin, ['rr'], [key])
    sin_rr(sn, 0.0, 'sn')
    TT(TG, PWi, mag, sn, ALU.mult, ['mag', 'sn'], ['PW'])
    sin_rr(cs, PI / 2, 'sn')
    TT(TG, PWr, mag, cs, ALU.mult, ['mag', 'sn'], ['PW'])
    ar = PWr[:, :, 8]; ai = PWi[:, :, 8]
    t1 = A.alloc([32], F32, 64); t2 = A.alloc([32], F32, 64); t3 = A.alloc([32], F32, 64)
    den = A.alloc([32], F32, 64); cr_ = A.alloc([32], F32, 64); ci_ = A.alloc([32], F32, 64); arm1 = A.alloc([32], F32, 64)
    RC, WC = ['PW', 'lr', 'li', 'cf'], ['cf']
    TT(TG, t1, lr, lr, ALU.mult, RC, WC)
    TT(TG, t2, li, li, ALU.mult, RC, WC)
    TT(TG, den, t1, t2, ALU.add, RC, WC)
    S.op('vector', lambda e: e.reciprocal(out=den, in_=den), RC, WC)
    TS(TG, arm1, ar, -1.0, None, ALU.add, None, RC, WC)
    TT(TG, t1, arm1, lr, ALU.mult, RC, WC)
    TT(TG, t2, ai, li, ALU.mult, RC, WC)
    TT(TG, t3, t1, t2, ALU.add, RC, WC)
    TT(TG, cr_, t3, den, ALU.mult, RC, WC)
    TT(TG, t1, ai, lr, ALU.mult, RC, WC)
    TT(TG, t2, arm1, li, ALU.mult, RC, WC)
    TT(TG, t3, t1, t2, ALU.subtract, RC, WC)
    TT(TG, ci_, t3, den, ALU.mult, RC, WC)
    u1 = tq.rearrange("p g k -> p (g k)")[:, 0:512].rearrange("p (g h) -> p g h", g=32)
    u2 = ang.rearrange("p g k -> p (g k)")[:, 0:512].rearrange("p (g h) -> p g h", g=32)
    SB_ = [64, 32, 16]
    crb = cr_.unsqueeze(2).to_broadcast(SB_); cib = ci_.unsqueeze(2).to_broadcast(SB_)
    RB, WB = ['cf', 'Br', 'Bi', 'bb', 'PW'], ['bb']
    TT(TG, u1, Br, crb, ALU.mult, RB, WB)
    TT(TG, u2, Bi, cib, ALU.mult, RB, WB)
    TT(TG, Bbr, u1, u2, ALU.subtract, RB, WB)
    TT(TG, u1, Bi, crb, ALU.mult, RB, WB)
    TT(TG, u2, Br, cib, ALU.mult, RB, WB)
    TT(TG, Bbi, u1, u2, ALU.add, RB, WB)
    CP(TG, Aa, PWr[:, :, 23:24].to_broadcast([64, 32, 2]), RB, WB)
    TS(TG, Ab[:, :, 0], PWi[:, :, 23], -1.0, None, ALU.mult, None, RB, WB)
    CP(TG, Ab[:, :, 1], PWi[:, :, 23], RB, WB)
    Aa2 = A2tab[:, :, 0:2]; Ab2 = A2tab[:, :, 2:4]; Aa4 = A2tab[:, :, 4:6]; Ab4 = A2tab[:, :, 6:8]
    for Aa_, Ab_, idx_ in ((Aa2, Ab2, 29), (Aa4, Ab4, 27)):
        CP(TG, Aa_, PWr[:, :, idx_:idx_ + 1].to_broadcast([64, 32, 2]), RB, WB)
        TS(TG, Ab_[:, :, 0], PWi[:, :, idx_], -1.0, None, ALU.mult, None, RB, WB)
        CP(TG, Ab_[:, :, 1], PWi[:, :, idx_], RB, WB)
    S.barrier()
    A.release(par_mark)

    GB = 8

    def alloc_tb():
        return dict(Hr=A.alloc([GB, 8, 16], F32, 64), Hi=A.alloc([GB, 8, 16], F32, 64),
                    Hrb=A.alloc([GB, 128], BF16, 64), Hib=A.alloc([GB, 128], BF16, 64),
                    w1=A.alloc([GB, 16, 16], F32, 64), w2=A.alloc([GB, 16, 16], F32, 64))

    RT, WT = ['PW', 'bb', 'Cr', 'Ci', 'tb'], ['tb']

    def gen_H(tb, g0):
        gs = slice(g0, g0 + GB)
        SHH = [64, GB, 8, 16]
        pwr_h = PWr[:, gs, 0:8].unsqueeze(3).to_broadcast(SHH)
        pwi_h = PWi[:, gs, 0:8].unsqueeze(3).to_broadcast(SHH)
        bbr_h = Bbr[:, gs, :].unsqueeze(2).to_broadcast(SHH)
        bbi_h = Bbi[:, gs, :].unsqueeze(2).to_broadcast(SHH)
        w1h = tb['w1'][:, :, 0:8, :]; w2h = tb['w2'][:, :, 0:8, :]
        Hr, Hi = tb['Hr'], tb['Hi']
        TT(TG, w1h, bbr_h, pwr_h, ALU.mult, RT, WT)
        TT(TG, w2h, bbi_h, pwi_h, ALU.mult, RT, WT)
        TT(TG, Hr, w1h, w2h, ALU.subtract, RT, WT)
        TT(TG, w1h, bbi_h, pwr_h, ALU.mult, RT, WT)
        TT(TG, w2h, bbr_h, pwi_h, ALU.mult, RT, WT)
        TT(TG, Hi, w1h, w2h, ALU.add, RT, WT)
        CP(TG, tb['Hrb'].rearrange("p g (s h) -> p g s h", s=8), Hr, RT, WT)
        CP(TG, tb['Hib'].rearrange("p g (s h) -> p g s h", s=8), Hi, RT, WT)

    def gen_F(tb, Fpm, g0):
        gs = slice(g0, g0 + GB)
        SHH = [64, GB, 8, 16]
        w1h = tb['w1'][:, :, 0:8, :]; w2h = tb['w2'][:, :, 0:8, :]
        Hr, Hi = tb['Hr'], tb['Hi']
        for b, idx in ((0, 23), (1, 15)):
            pr = PWr[:, gs, idx:idx + 1].unsqueeze(3).to_broadcast(SHH)
            pi_ = PWi[:, gs, idx:idx + 1].unsqueeze(3).to_broadcast(SHH)
            fre = Fpm[:, :, b, 0, :].rearrange("p g (s h) -> p g s h", s=8)
            fim = Fpm[:, :, b, 1, :].rearrange("p g (s h) -> p g s h", s=8)
            TT(TG, w1h, Hr, pr, ALU.mult, RT, WT)
            TT(TG, w2h, Hi, pi_, ALU.mult, RT, WT)
            TT(TG, fre, w1h, w2h, ALU.subtract, RT, WT)
            TT(TG, w1h, Hi, pr, ALU.mult, RT, WT)
            TT(TG, w2h, Hr, pi_, ALU.mult, RT, WT)
            TT(TG, fim, w1h, w2h, ALU.add, RT, WT)
        for gl in range(GB):
            g = g0 + gl
            pb2 = 2 + g % 2
            TRS([(PSB(pb2)[:, (b * 2 + ri) * 64:(b * 2 + ri + 1) * 64], Fpm[:, gl, b, ri, :], 64)
                 for b in range(2) for ri in range(2)], ['tb'], [pk(pb2)])
            CP('scalar', FT[:, g, :, :, :].rearrange("p b r n -> p (b r n)"), PSB(pb2)[:, 0:256], [pk(pb2)], ['FT'])

    def gen_E(tb, Etb, g0, ek='Et'):
        gs = slice(g0, g0 + GB)
        SHE = [64, GB, 16, 16]
        w1, w2 = tb['w1'], tb['w2']
        pre = PWr[:, gs, 8:24].unsqueeze(3).to_broadcast(SHE)
        pie = PWi[:, gs, 8:24].unsqueeze(3).to_broadcast(SHE)
        cre = Cr[:, gs, :].unsqueeze(2).to_broadcast(SHE)
        cie = Ci[:, gs, :].unsqueeze(2).to_broadcast(SHE)
        ere = Etb[:, :, 0, :].rearrange("p g (t h) -> p g t h", t=16)
        eim = Etb[:, :, 1, :].rearrange("p g (t h) -> p g t h", t=16)
        TT(TG, w1, cre, pre, ALU.mult, RT, WT)
        TT(TG, w2, cie, pie, ALU.mult, RT, WT)
        TT(TG, ere, w1, w2, ALU.subtract, RT, ['tb', ek])
        TT(TG, w1, cre, pie, ALU.mult, RT, WT)
        TT(TG, w2, cie, pre, ALU.mult, RT, WT)
        STT(TG, eim, w1, -1.0, w2, ALU.mult, ALU.subtract, RT, ['tb', ek])

    def gen_Dmm(tb, Dtb, Etb, g0, ek='Et', dk='Dt'):
        for gl in range(GB):
            g = g0 + gl
            pb = g % 2
            MMS([(PS(pb)[:, 0:256], [(tb['Hrb'][:, gl, :], Etb[:, gl, 0, :]), (tb['Hib'][:, gl, :], Etb[:, gl, 1, :])])],
                ['tb', ek], [pk(pb)])
            TT('vector', Dtb[:, gl, :], PS(pb)[:, 0:256], cmask[:, 2, :], ALU.mult, [pk(pb), 'cmask'], [dk])
            STT('vector', Dtb[:, gl, 0:128], ident_f, dcol[:, g:g + 1], Dtb[:, gl, 0:128], ALU.mult, ALU.add,
                [dk, 'ident_f', 'dcol'], [dk])

    def gen_DE(tb, Dtb, Etb, g0):
        gen_E(tb, Etb, g0)
        gen_Dmm(tb, Dtb, Etb, g0)

    tb_mark = A.mark()
    tb = alloc_tb()
    Fpm = A.alloc([GB, 2, 2, 128], BF16, 64)
    for gb in range(4):
        gen_H(tb, gb * GB)
        gen_F(tb, Fpm, gb * GB)
    if stop_after == 'tables':
        Dtb = A.alloc([GB, 256], BF16); Etb = A.alloc([GB, 2, 256], BF16, 64)
        gen_H(tb, 8)
        gen_DE(tb, Dtb, Etb, 8)
        Dtf = A.alloc([GB, 256], F32)
        CP('vector', Dtf, Dtb, ['Dt'], ['Dtf'])
        dbg_out('dt', Dtf, 'Dtf')
        FTf = A.alloc([32, 2, 2, 64], F32)
        CP('vector', FTf, FT, ['FT'], ['FTf'])
        dbg_out('ft', FTf, 'FTf')
        dbg_out('pwr', PWr, 'PW'); dbg_out('pwi', PWi, 'PW')
        return finish(nc, es, S)
    S.barrier()
    A.release(tb_mark)

    rt1 = A.alloc([32, 2], F32, 64); rt2 = A.alloc([32, 2], F32, 64)
    redA = A.alloc([4, 16, 8], F32, 64); redB = A.alloc([4, 16, 8], F32, 64)
    Yk = A.alloc([32, 2, 16], F32, 64); Zc = A.alloc([32, 2], F32, 64)
    SV = A.alloc([32, 2, 137], F32, 64)
    S.op('gpsimd', lambda e: e.memset(SV[:, :, :, 0:1], 0.0), [], ['SVc'])

    def piece_part2(pc):
        ntile, xn, xk = piece_cfg(pc)
        nch = ntile * 8
        while pending_red:
            pending_red.pop(0)()
        for g in range(32):
            pb = 6 + g % 2
            MMS([(PS(pb, 64)[:, ri * nch:(ri + 1) * nch], [(FT[:, g, b, ri, :], UT[:, g, b, 0:nch]) for b in range(2)])
                 for ri in range(2)], ['FT', 'UT'], [pk(pb)])
            CP(evac_eng(), SV[:, g, :, 1:1 + nch], PS(pb, 64)[:, 0:2 * nch].rearrange("p (r c) -> p r c", r=2),
               [pk(pb)], ['SV'])
        if pc < 3:
            def red_quarter(gq):
                gs = slice(4 * gq, 4 * gq + 4)
                SH5 = [64, 4, 16, 8]
                vr = SV[:, gs, 0, 1:129].rearrange("p g (k j) -> p g k j", j=8)
                vi = SV[:, gs, 1, 1:129].rearrange("p g (k j) -> p g k j", j=8)
                w1r = PWr[:, gs, 24:32].unsqueeze(2).to_broadcast(SH5)
                w1i = PWi[:, gs, 24:32].unsqueeze(2).to_broadcast(SH5)
                RR, WR = ['SV', 'PW', 'red'], ['red']
                TT('vector', redA, vr, w1r, ALU.mult, RR, WR)
                TT('vector', redB, vi, w1i, ALU.mult, RR, WR)
                TT('vector', redA, redA, redB, ALU.subtract, RR, WR)
                S.op('vector', lambda e: e.tensor_reduce(out=Yk[:, gs, 0, :], in_=redA, axis=mybir.AxisListType.X, op=ALU.add), RR, ['red', 'Yk'])
                TT('vector', redA, vr, w1i, ALU.mult, RR, WR)
                TT('vector', redB, vi, w1r, ALU.mult, RR, WR)
                TT('vector', redA, redA, redB, ALU.add, RR, WR)
                S.op('vector', lambda e: e.tensor_reduce(out=Yk[:, gs, 1, :], in_=redA, axis=mybir.AxisListType.X, op=ALU.add), RR, ['red', 'Yk'])

            def red_outer():
                y_r = Yk[:, :, 0, :]; y_i = Yk[:, :, 1, :]
                w2r = PWr[:, :, 32:48]; w2i = PWi[:, :, 32:48]
                r2a = redA.rearrange("p g k j -> p (g k j)")[:, 0:512].rearrange("p (g k) -> p g k", g=32)
                r2b = redB.rearrange("p g k j -> p (g k j)")[:, 0:512].rearrange("p (g k) -> p g k", g=32)
                RR, WR = ['Yk', 'PW', 'red'], ['red']
                TT('vector', r2a, y_r, w2r, ALU.mult, RR, WR)
                TT('vector', r2b, y_i, w2i, ALU.mult, RR, WR)
                TT('vector', r2a, r2a, r2b, ALU.subtract, RR, WR)
                S.op('vector', lambda e: e.tensor_reduce(out=Zc[:, :, 0], in_=r2a, axis=mybir.AxisListType.X, op=ALU.add), RR, ['red', 'Zc'])
                TT('vector', r2a, y_r, w2i, ALU.mult, RR, WR)
                TT('vector', r2b, y_i, w2r, ALU.mult, RR, WR)
                TT('vector', r2a, r2a, r2b, ALU.add, RR, WR)
                S.op('vector', lambda e: e.tensor_reduce(out=Zc[:, :, 1], in_=r2a, axis=mybir.AxisListType.X, op=ALU.add), RR, ['red', 'Zc'])
                Xin = SV[:, :, :, 0]
                a128r = PWr[:, :, 48]; a128i = PWi[:, :, 48]
                RC_, WC_ = ['Zc', 'SVc', 'PW', 'rt'], ['rt']
                TT('vector', rt1[:, :, 0], Xin[:, :, 0], a128r, ALU.mult, RC_, WC_)
                TT('vector', rt1[:, :, 1], Xin[:, :, 1], a128i, ALU.mult, RC_, WC_)
                TT('vector', rt2[:, :, 0], Xin[:, :, 1], a128r, ALU.mult, RC_, WC_)
                TT('vector', rt2[:, :, 1], Xin[:, :, 0], a128i, ALU.mult, RC_, WC_)
                TT('vector', rt1[:, :, 0], rt1[:, :, 0], rt1[:, :, 1], ALU.subtract, RC_, WC_)
                TT('vector', rt2[:, :, 0], rt2[:, :, 0], rt2[:, :, 1], ALU.add, RC_, WC_)
                TT('vector', SV[:, :, 0, 0], Zc[:, :, 0], rt1[:, :, 0], ALU.add, RC_, ['SVc'])
                TT('vector', SV[:, :, 1, 0], Zc[:, :, 1], rt2[:, :, 0], ALU.add, RC_ + ['SVc'], ['SVc'])

            for gq in range(8):
                pending_red.append(lambda gq=gq: red_quarter(gq))
            pending_red.append(red_outer)
        else:
            while pending_red:
                pending_red.pop(0)()
            tmpv = Ucm.rearrange("p g t h -> p (g t h)")[0:64, :].bitcast(F32)
            NCH_ = 17

            def cmac(dst0, src0, step, count, Aa_, Ab_):
                for o_ in range(0, count, NCH_):
                    n_ = min(NCH_, count - o_)
                    d0 = dst0 + step * o_
                    s0 = src0 + step * o_
                    dst = SV[:, :, :, d0:d0 + step * (n_ - 1) + 1:step]
                    src = SV[:, :, :, s0:s0 + step * (n_ - 1) + 1:step]
                    T1 = tmpv[:, 0:64 * n_].rearrange("p (g r n) -> p g r n", g=32, r=2)
                    T2 = tmpv[:, 2048:2048 + 64 * n_].rearrange("p (g r n) -> p g r n", g=32, r=2)
                    SHc = [64, 32, 2, n_]
                    kin = ['SV', 'SVc', 'bb', 'Ucm']
                    TT('vector', T1, src, Aa_.unsqueeze(3).to_broadcast(SHc), ALU.mult, kin, ['Ucm'])
                    TT('vector', T2[:, :, 0, :], src[:, :, 1, :], Ab_[:, :, 0:1].to_broadcast([64, 32, n_]), ALU.mult, kin, ['Ucm'])
                    TT('vector', T2[:, :, 1, :], src[:, :, 0, :], Ab_[:, :, 1:2].to_broadcast([64, 32, n_]), ALU.mult, kin, ['Ucm'])
                    TT('vector', dst, dst, T1, ALU.add, ['Ucm', 'SV'], ['SV'])
                    TT('vector', dst, dst, T2, ALU.add, ['Ucm', 'SV'], ['SV'])

            cmac(2, 1, 2, 68, Aa, Ab)
            cmac(4, 2, 4, 34, Aa2, Ab2)
            for c in range(0, nch, 4):
                Xp = SV[:, :, :, c]
                Xn = SV[:, :, :, c + 4]
                kin = ['SV', 'SVc', 'bb']
                TT('vector', rt1, Xp, Aa4, ALU.mult, kin, ['rt1'])
                TT('vector', rt2[:, :, 0], Xp[:, :, 1], Ab4[:, :, 0], ALU.mult, kin, ['rt2a'])
                TT('vector', rt2[:, :, 1], Xp[:, :, 0], Ab4[:, :, 1], ALU.mult, kin, ['rt2b'])
                TT('vector', Xn, Xn, rt1, ALU.add, ['rt1', 'SV'], ['SV'])
                TT('vector', Xn, Xn, rt2, ALU.add, ['rt2a', 'rt2b', 'SV'], ['SV'])
            cmac(2, 0, 4, 34, Aa2, Ab2)
            cmac(1, 0, 2, 68, Aa, Ab)

    piece_part2(0)
    for pc in range(1, 4):
        piece_part1(pc)
        piece_part2(pc)
    CP('vector', SPb, SV[:, :, :, 0:136], ['SV', 'SVc', 'FT'], ['SPb', 'FT'])
    if 'sv' in dbg_d:
        dbg_out('sv', SV, 'SV')
    S.barrier()
    A.release(y_mark)

    yT = A.alloc_top([4, OWN17], BF16)
    tb = alloc_tb()
    Dtb2 = [A.alloc([GB, 256], BF16) for _ in range(2)]
    Etb2 = [A.alloc([GB, 2, 256], BF16, 64) for _ in range(2)]
    Yg = A.alloc([2, 16, 128], BF16)
    gen_H(tb, 0)
    gen_E(tb, Etb2[0], 0, ek=('Et', 0))
    gen_Dmm(tb, Dtb2[0], Etb2[0], 0, ek=('Et', 0), dk=('Dt', 0))
    for gb in range(4):
        g0 = gb * GB
        Dtb, Etb = Dtb2[gb % 2], Etb2[gb % 2]
        dk_, ek_ = ('Dt', gb % 2), ('Et', gb % 2)
        if gb + 1 < 4:
            nb_ = (gb + 1) % 2
            gen_H(tb, g0 + GB)
            gen_E(tb, Etb2[nb_], g0 + GB, ek=('Et', nb_))
        for cti, (c0, ncc) in enumerate(OWN_CTS):
            for gp in range(4):
                pb = 4 + gp % 2
                steps = []
                for j in range(2):
                    gl = gp * 2 + j
                    g = g0 + gl
                    o = PS(pb, ncc)[:, j * 256:(j + 1) * 256]
                    steps.append((o, UT[:, g, 0, c0:c0 + ncc], Dtb[:, gl, :], True, False))
                    steps.append((o[:, 128:256], UT[:, g, 1, c0:c0 + ncc], Dtb[:, gl, 0:128], False, False))
                    steps.append((o, SPb[:, g, 0, c0:c0 + ncc], Etb[:, gl, 0, :], False, False))
                    steps.append((o, SPb[:, g, 1, c0:c0 + ncc], Etb[:, gl, 1, :], False, True))
                MMRAW(steps, ['UT', dk_, 'SPb', ek_], [pk(pb)])
                ACT(Yg[0:ncc, cti, :, gp * 32:gp * 32 + 32].rearrange("p t (g h) -> p t g h", g=2),
                    PS(pb, ncc).rearrange("p (g t h) -> p t g h", g=2, t=16), AF.Gelu_apprx_tanh, [pk(pb)], ['Yg'])
        for cti, (c0, ncc) in enumerate(OWN_CTS):
            for th in range(2):
                pb = 6 + th
                TRS([(PSB(pb)[:, tt * ncc:(tt + 1) * ncc], Yg[0:ncc, cti, th * 8 + tt, :], ncc) for tt in range(8)],
                    ['Yg'], [pk(pb)])
                dst = yT[:, gb, :].rearrange("p (c t) -> p t c", t=16)[:, th * 8:th * 8 + 8, c0:c0 + ncc]
                CP(evac_eng(), dst, PSB(pb)[:, 0:8 * ncc].rearrange("p (t c) -> p t c", t=8), [pk(pb)], ['yT'])
        if gb + 1 < 4:
            gen_Dmm(tb, Dtb2[nb_], Etb2[nb_], g0 + GB, ek=('Et', nb_), dk=('Dt', nb_))
    S.barrier()
    A.release(a_mark)
    if 'yT' in dbg_d:
        yTf = A.alloc([4, OWN17], F32)
        CP('vector', yTf, yT, ['yT'], ['yTf'])
        dbg_out('yT', yTf, 'yTf')
        S.barrier()
        A.release(a_mark)
    if stop_after == 'ssm':
        return finish(nc, es, S)

    BLKS = [(0, 512), (512, 512), (1024, 512), (1536, 512), (2048, 128)]
    wglu = A.alloc([4, 512], BF16)
    gluB = A.alloc([4], F32)
    sg = A.alloc([4, 512], BF16)
    wload(wglu, glu_w.rearrange("(k p) n -> p k n", p=128), 'wglu')
    S.dma('sync', gluB, glu_b, writes=['gluB'])
    for bi, (o, n) in enumerate(BLKS):
        yk = ('yT', bi)
        for j in range(4):
            MMS([(PS(j)[:, 0:n], [(wglu[:, k, j * 128:(j + 1) * 128], yT[:, k, o:o + n]) for k in range(4)])],
                [yk, 'wglu'], [pk(j)])
        for j in range(4):
            ACT(sg[:, j, 0:n], PS(j)[:, 0:n], AF.Sigmoid, [pk(j), 'gluB'], [('sg', j)], bias=gluB[:, j:j + 1])
            TT('vector', yT[:, j, o:o + n], yT[:, j, o:o + n], sg[:, j, 0:n], ALU.mult, [('sg', j), yk], [yk])
    S.barrier()
    A.release(a_mark)
    if 'ya' in dbg_d:
        yTf = A.alloc([4, OWN17], F32)
        CP('vector', yTf, yT, [], ['yTf'])
        dbg_out('ya', yTf, 'yTf')
        S.barrier()
        A.release(a_mark)

    if stop_after == 'glu':
        return finish(nc, es, S)

    OT = A.alloc_top([4, OWN17], BF16, 64)
    acc = A.alloc([2, OWN17], F32)
    rden = A.alloc([512], F32, 64)
    wq = A.alloc([8, 128], BF16); wk = A.alloc([8, 128], BF16); wv = A.alloc([8, 128], BF16)
    wqp = True; wkp = True
    QT = A.alloc([OWN17], BF16)
    KT = A.alloc([OWN17 + 2048], BF16)
    VT = A.alloc([OWN17 + 2048], BF16)
    Vt3 = A.alloc([48, 3, 64], BF16)
    vld = A.alloc([64], F32)
    NRB = 3
    rtb = [A.alloc([2, 512], F32, 96) for _ in range(NRB)]
    rtmp = [A.alloc([512], F32, 96) for _ in range(2)]
    NPB = 4
    Pe = [A.alloc([2, 128], BF16) for _ in range(NPB)]
    Pm = [A.alloc([2, 128], BF16) for _ in range(NPB)]
    for pe_ in Pe:
        S.op('gpsimd', lambda e, pe_=pe_: e.memset(pe_, 0.0), [], [('Pe', Pe.index(pe_))])
    rcnt = [0]
    pcnt = [0]

    def proj_nat(dstT, dkey, col0, n, wmain, wperm, rhs_list, rcol0):
        i = rcnt[0] % 2
        ib = rcnt[0] % NRB
        rcnt[0] += 1
        pbm = 0 + i
        pbp = 2 + i
        MMS([(PS(pbm)[:, 0:n], [(wmain[:, k, :], rhs_list[k]) for k in range(8)])], ['xnA', 'xnB', 'watt'], [pk(pbm)])
        CP('scalar', dstT[:, col0:col0 + n], PS(pbm)[:, 0:n], [pk(pbm)], [dkey])
        if wperm is None:
            return
        MMS([(PS(pbp)[:, 0:n], [(permT, dstT[:, col0:col0 + n])])], [dkey, 'permT'], [pk(pbp)])
        S.dma('sync', rtb[ib][:, :, 0:n], rot_n[:, :, rcol0:rcol0 + n], writes=[('rtb', ib)])
        TT('vector', rtmp[i][:, 0:n], PS(pbm, 96)[:, 0:n], rtb[ib][:, 0, 0:n], ALU.mult, [pk(pbm), ('rtb', ib), dkey], [('rtmp', i)])
        TT('vector', rtb[ib][:, 1, 0:n], PS(pbp, 96)[:, 0:n], rtb[ib][:, 1, 0:n], ALU.mult, [pk(pbp), ('rtb', ib)], [('rtb', ib)])
        TT('vector', dstT[0:96, col0:col0 + n], rtmp[i][:, 0:n], rtb[ib][:, 1, 0:n], ALU.add,
           [('rtmp', i), ('rtb', ib), dkey], [dkey])

    def chunks(total, step=512):
        o = 0
        while o < total:
            yield o, min(step, total - o)
            o += step

    for hp in range(2):
        for gi, dil in enumerate(DILS):
            hA = 4 * gi + 2 * hp
            per_q = OWN17 // dil
            per_k = per_q + 128
            nblk = (per_k + 127) // 128
            halo = 128 * dil
            cq = 512 + 64 * hA
            ck = 1280 + 64 * hA
            cv = 2048 + 64 * hA
            for dst, c0_ in ((wq, cq), (wk, ck), (wv, cv)):
                wload(dst, w_in[:, c0_:c0_ + 128].rearrange("(k p) n -> p k n", p=128), 'watt')
            S.dma('sync', vld, valid_d[gi], writes=['vld'])
            CP('vector', Vt3[:, 0:dil * nblk, 1, :], vld[:, 0:dil * nblk].unsqueeze(2).to_broadcast([128, dil * nblk, 64]),
               ['vld', 'Vt'], ['Vt'])
            for o, n in chunks(halo):
                rl = [xnA[:, k, OWN - halo + o:OWN - halo + o + n] for k in range(8)]
                proj_nat(KT, 'KT', o, n, wk, wkp, rl, 2048 - halo + o)
                proj_nat(VT, 'VT', o, n, wv, None, rl, 0)
            for o, n in chunks(OWN17):
                rl = [xnB[:, k, o:o + n] for k in range(8)]
                proj_nat(KT, 'KT', halo + o, n, wk, wkp, rl, 2048 + o)
                proj_nat(QT, 'QT', o, n, wq, wqp, rl, 2048 + o)
                proj_nat(VT, 'VT', halo + o, n, wv, None, rl, 0)
            blist = [(r, j) for j in range(nblk) for r in range(dil)]
            for b0 in range(0, len(blist), 8):
                grp = blist[b0:b0 + 8]
                pb = 4 + (pcnt[0] % 2)
                pcnt[0] += 1
                items = []
                for jj, (r, j) in enumerate(grp):
                    nkb = min(128, per_k - 128 * j)
                    st = r + dil * 128 * j
                    items.append((PSB(pb, nkb)[:, jj * 128:(jj + 1) * 128], VT[:, st:st + dil * (nkb - 1) + 1:dil], 128))
                TRS(items, ['VT'], [pk(pb)])
                nkbs = [min(128, per_k - 128 * j) for (r, j) in grp]
                ee = evac_eng()
                jj = 0
                while jj < len(grp):
                    je = jj
                    while je < len(grp) and nkbs[je] == nkbs[jj]:
                        je += 1
                    nkb = nkbs[jj]
                    CP(ee, Vt3[0:nkb, b0 + jj:b0 + je, 0:3:2, :],
                       PSB(pb, nkb)[:, 128 * jj:128 * je].rearrange("p (j u d) -> p j u d", u=2, d=64), [pk(pb)], ['Vt'])
                    jj = je
            tiles = [(hj, r, t) for hj in range(2) for r in range(dil) for t in range((per_q + 127) // 128)]

            def att_scores(hj, r, t, i):
                rows = slice(64 * hj, 64 * hj + 64)
                nq = min(128, per_q - 128 * t)
                nk2 = min(128, per_k - 128 * (t + 1))
                pbs = 4 + i
                q0 = r + dil * 128 * t
                qsl = QT[rows, q0:q0 + dil * (nq - 1) + 1:dil]
                k1 = r + dil * 128 * t
                k2 = r + dil * 128 * (t + 1)
                MMS([(PS(pbs)[:, 0:nq], [(KT[rows, k1:k1 + dil * 127 + 1:dil], qsl)]),
                     (PS(pbs, nk2)[:, 128:128 + nq], [(KT[rows, k2:k2 + dil * (nk2 - 1) + 1:dil], qsl)])],
                    ['KT', 'QT'], [pk(pbs)])

            def att_rest(hj, r, t, i, ip):
                nq = min(128, per_q - 128 * t)
                nk2 = min(128, per_k - 128 * (t + 1))
                pbs = 4 + i
                pbo = 6 + i
                ACT(Pe[ip], PS(pbs)[:, 0:256].rearrange("p (u n) -> p u n", u=2), AF.Exp, [pk(pbs)], [('Pe', ip)], scale=0.125)
                TT('vector', Pm[ip], Pe[ip], cmask[:, 0:2, 0:128], ALU.mult,
                   [('Pe', ip), 'cmask'], [('Pm', ip)])
                b1 = r * nblk + t
                b2 = b1 + 1
                vc = slice(64 * hj, 64 * hj + 64)
                MMS([(PS(pbo, 64)[:, 0:nq], [(Vt[:, b1, vc], Pm[ip][:, 0, 0:nq]), (Vt[0:nk2, b2, vc], Pm[ip][0:nk2, 1, 0:nq])]),
                     (PS(pbo, 64)[:, 128:128 + nq], [(VO[:, b1, :], Pm[ip][:, 0, 0:nq]), (VO[0:nk2, b2, :], Pm[ip][0:nk2, 1, 0:nq])])],
                    [('Pm', ip), 'Vt', 'VO'], [pk(pbo)])
                t0_ = r + dil * 128 * t
                dsta = acc[:, hj, :, t0_:t0_ + dil * (nq - 1) + 1:dil]
                srcp = PS(pbo, 64)[:, 0:256].rearrange("p (u n) -> p u n", u=2)[:, :, 0:nq]
                if gi == 0:
                    CP('vector', dsta, srcp, [pk(pbo)], ['acc'])
                else:
                    TT('vector', dsta, dsta, srcp, ALU.add, [pk(pbo), 'acc'], ['acc'])

            def att_exp(hj, r, t, i, ip):
                pbs = 4 + i
                nq = min(128, per_q - 128 * t)
                nk2 = min(128, per_k - 128 * (t + 1))
                if nq == 128 and nk2 == 128:
                    ACT(Pe[ip], PS(pbs)[:, 0:256].rearrange("p (u n) -> p u n", u=2), AF.Exp, [pk(pbs)], [('Pe', ip)], scale=0.125)
                else:
                    ACT(Pe[ip][:, 0, 0:nq], PS(pbs)[:, 0:nq], AF.Exp, [pk(pbs)], [('Pe', ip)], scale=0.125)
                    ACT(Pe[ip][0:nk2, 1, 0:nq], PS(pbs, nk2)[:, 128:128 + nq], AF.Exp, [pk(pbs), ('Pe', ip)], [('Pe', ip)], scale=0.125)
                TT('vector', Pm[ip], Pe[ip], cmask[:, 0:2, 0:128], ALU.mult,
                   [('Pe', ip), 'cmask'], [('Pm', ip)])

            def att_pv(hj, r, t, i, ip):
                nq = min(128, per_q - 128 * t)
                nk2 = min(128, per_k - 128 * (t + 1))
                pbo = 6 + i
                b1 = t * dil + r
                b2 = (t + 1) * dil + r
                vb1 = Vt3[:, b1, :, :].rearrange("p u d -> p (u d)")[:, 64 * hj:64 * hj + 128]
                vb2 = Vt3[0:nk2, b2, :, :].rearrange("p u d -> p (u d)")[:, 64 * hj:64 * hj + 128]
                MMS([(PS(pbo)[:, 0:nq], [(vb1, Pm[ip][:, 0, 0:nq]), (vb2, Pm[ip][0:nk2, 1, 0:nq])])],
                    [('Pm', ip), 'Vt'], [pk(pbo)])

            def att_acc(hj, r, t, i, ip):
                nq = min(128, per_q - 128 * t)
                pbo = 6 + i
                t0_ = r + dil * 128 * t
                dsta = acc[:, hj, t0_:t0_ + dil * (nq - 1) + 1:dil]
                srcp = PS(pbo)[:, 0:nq]
                if gi == 0:
                    CP('vector', dsta, srcp, [pk(pbo)], ['acc'])
                else:
                    TT('vector', dsta, dsta, srcp, ALU.add, [pk(pbo), 'acc'], ['acc'])

            NTL = len(tiles)
            att_scores(*tiles[0], 0)
            if NTL > 1:
                att_scores(*tiles[1], 1)
            att_exp(*tiles[0], 0, 0)
            for n_ in range(NTL):
                att_pv(*tiles[n_], n_ % 2, n_ % NPB)
                if n_ + 2 < NTL:
                    att_scores(*tiles[n_ + 2], n_ % 2)
                if n_ + 1 < NTL:
                    att_exp(*tiles[n_ + 1], (n_ + 1) % 2, (n_ + 1) % NPB)
                att_acc(*tiles[n_], n_ % 2, n_ % NPB)
        for hj in range(2):
            for bi_, (o, n) in enumerate(BLKS):
                pbf = bi_ % 2
                MMS([(PS(pbf, 64)[:, 0:n], [(shiftD, acc[:, hj, o:o + n])])], ['acc', 'shiftD'], [pk(pbf)])
                if hj == 0:
                    TS('vector', rden[:, 0:n], PS(pbf, 64)[:, 0:n], 1e-30, None, ALU.max, None, [pk(pbf)], ['rden'])
                    S.op('vector', lambda e, n=n: e.reciprocal(out=rden[:, 0:n], in_=rden[:, 0:n]), ['rden'], ['rden'])
                    TT('vector', OT[:, 2 * hp + hj, o:o + n], acc[0:64, hj, o:o + n], rden[:, 0:n], ALU.mult, ['acc', 'rden'], ['OT'])
                else:
                    TS('vector', rden[:, 0:n], acc[0:64, hj, o:o + n], 1e-30, None, ALU.max, None, ['acc'], ['rden'])
                    S.op('vector', lambda e, n=n: e.reciprocal(out=rden[:, 0:n], in_=rden[:, 0:n]), ['rden'], ['rden'])
                    TT('vector', OT[:, 2 * hp + hj, o:o + n], PS(pbf, 64)[:, 0:n], rden[:, 0:n], ALU.mult, [pk(pbf), 'rden'], ['OT'])
    S.barrier()
    A.release(a_mark)
    if 'att' in dbg_d:
        otf = A.alloc([4, OWN17], F32, 64)
        CP('vector', otf, OT, [], ['otf'])
        dbg_out('att', otf, 'otf')
        S.barrier()
        A.release(a_mark)

    if stop_after == 'att':
        return finish(nc, es, S)

    A.release_top(A.mark_top())
    mixT = A.alloc([8, OWN17], BF16)
    b_mark0 = A.mark()
    gateB = A.alloc([16], F32)
    S.dma('sync', gateB, gate_b, writes=['gateB'])
    wga = [A.alloc([8, 128], BF16) for _ in range(2)]
    wgb = [A.alloc([8, 128], BF16) for _ in range(2)]
    wba = [A.alloc([4, 128], BF16) for _ in range(2)]
    wbb = [A.alloc([4, 128], BF16, 64) for _ in range(2)]
    sga = A.alloc([512], F32); sgb = A.alloc([512], F32)
    ta = A.alloc([512], F32); tbb_ = A.alloc([512], F32)
    for m in range(8):
        wi = m % 2
        wk_ = ('wep', wi)
        wload(wga[wi], w_in[:, 2816 + 128 * m:2816 + 128 * (m + 1)].rearrange("(k p) n -> p k n", p=128), wk_)
        wload(wgb[wi], w_in[:, 3840 + 128 * m:3840 + 128 * (m + 1)].rearrange("(k p) n -> p k n", p=128), wk_)
        wload(wba[wi], w_ba[:, 128 * m:128 * (m + 1)].rearrange("(k p) n -> p k n", p=128), wk_)
        wload(wbb[wi], w_bb[:, 128 * m:128 * (m + 1)].rearrange("(k p) n -> p k n", p=64), wk_)
        for (o, n) in BLKS:
            MMS([(PS(0)[:, 0:n], [(wga[wi][:, k, :], xnB[:, k, o:o + n]) for k in range(8)])], [wk_], [pk(0)])
            ACT(sga[:, 0:n], PS(0)[:, 0:n], AF.Sigmoid, [pk(0), 'gateB'], ['sga'], bias=gateB[:, m:m + 1])
            MMS([(PS(1)[:, 0:n], [(wba[wi][:, k, :], yT[:, k, o:o + n]) for k in range(4)])], [wk_], [pk(1)])
            TT('vector', ta[:, 0:n], PS(1)[:, 0:n], sga[:, 0:n], ALU.mult, [pk(1), 'sga'], ['ta'])
            MMS([(PS(2)[:, 0:n], [(wgb[wi][:, k, :], xnB[:, k, o:o + n]) for k in range(8)])], [wk_], [pk(2)])
            ACT(sgb[:, 0:n], PS(2)[:, 0:n], AF.Sigmoid, [pk(2), 'gateB'], ['sgb'], bias=gateB[:, 8 + m:9 + m])
            MMS([(PS(3)[:, 0:n], [(wbb[wi][:, h_, :], OT[:, h_, o:o + n]) for h_ in range(4)])], [wk_], [pk(3)])
            TT('vector', tbb_[:, 0:n], PS(3)[:, 0:n], sgb[:, 0:n], ALU.mult, [pk(3), 'sgb'], ['tb2'])
            TT('vector', mixT[:, m, o:o + n], ta[:, 0:n], tbb_[:, 0:n], ALU.add, ['ta', 'tb2'], ['mixT'])
    S.barrier()
    A.release(b_mark0)
    A.release_top(topA)
    if 'mix' in dbg_d:
        mf = A.alloc([8, OWN17], F32)
        CP('vector', mf, mixT, [], ['mf'])
        dbg_out('mix', mf, 'mf')
        S.barrier()
        A.release(b_mark0)
    if stop_after == 'mix':
        return finish(nc, es, S)

    topB = A.mark_top()
    h_all = A.alloc_top([16, D], F32)
    u2T = A.alloc_top([8, OWN], BF16)
    u2h = A.alloc_top([8, 2], BF16)
    carry = A.alloc_top([NF, 2], F32)
    cw = A.alloc_top([NF, 3], F32); cb = A.alloc_top([NF], F32)
    gffn = A.alloc_top([8], F32); gple = A.alloc_top([8], F32)
    for dst, src, key in ((gffn, g_ffn, 'gffn'), (gple, g_ple, 'gple'), (cw, conv_w, 'cw'), (cb, conv_b, 'cb')):
        S.dma('sync', dst, src, writes=[key])
    b1_mark = A.mark()
    wo = A.alloc([8, D], BF16)
    wload(wo, w_out.rearrange("(k p) n -> p k n", p=128), 'wo')
    xtl = [A.alloc([D], F32) for _ in range(2)]
    hhalo = A.alloc([D], F32)
    uhalo = A.alloc([8, 128], BF16)
    nscr2 = [alloc_nscr('n2a'), alloc_nscr('n2b', share_sq=True)]
    def b1_mm(tl):
        tok = tl * 128
        bi = tl % 2
        for half in range(2):
            MMS([(PS(2 * bi + half), [(mixT[:, k, tok:tok + 128], wo[:, k, half * 512:(half + 1) * 512]) for k in range(8)])],
                ['wo'], [pk(2 * bi + half)])
        S.dma('sync', xtl[bi], xe[6144 + tok:6144 + tok + 128, :], writes=[('xtl', bi)])

    def b1_rest(tl):
        bi = tl % 2
        xk_ = ('xtl', bi)
        hdst = hhalo if tl == 0 else h_all[:, tl - 1, :]
        hk = ('h', tl)
        for half in range(2):
            TT('vector', hdst[:, half * 512:(half + 1) * 512], PS(2 * bi + half), xtl[bi][:, half * 512:(half + 1) * 512], ALU.add,
               [pk(2 * bi + half), xk_], [hk])
        if tl == 0:
            norm_T(hdst, hk, uhalo, 'uhalo', gffn, 'gffn', nscr2[0], 4)
            CP('vector', u2h, uhalo[:, :, 126:128], ['uhalo'], ['u2h'])
        else:
            norm_T(hdst, hk, u2T[:, :, (tl - 1) * 128:tl * 128], ('u2T', (tl - 1) // 2), gffn, 'gffn', nscr2[tl % 2], 4 + tl % 2)

    b1_mm(0)
    for tl in range(17):
        if tl + 1 < 17:
            b1_mm(tl + 1)
        b1_rest(tl)
    S.barrier()
    A.release(a_mark)
    NWB = 4
    wgB = [A.alloc([8, 256], BF16) for _ in range(NWB)]
    wuB = [A.alloc([8, 256], BF16) for _ in range(NWB)]
    wdB = [A.alloc([2, D], BF16) for _ in range(NWB)]
    NG_ = 4
    Gs = [A.alloc([258], F32) for _ in range(2)] * 2
    c1 = [A.alloc([256], F32) for _ in range(2)] * 2
    gl_ = [A.alloc([256], F32) for _ in range(2)] * 2
    AT = [A.alloc([256], BF16) for _ in range(NG_)]
    gcnt = [0]

    def ffn_load(fp):
        wi = fp % NWB
        wkB = ('wB', wi)
        wload(wgB[wi], w_gate[:, 256 * fp:256 * (fp + 1)].rearrange("(k p) n -> p k n", p=128), wkB)
        wload(wuB[wi], w_up[:, 256 * fp:256 * (fp + 1)].rearrange("(k p) n -> p k n", p=128), wkB)
        wload(wdB[wi], w_down[256 * fp:256 * (fp + 1), :].rearrange("(j p) n -> p j n", p=128), wkB)

    def ffn_gu(st):
        fp, u, par = st['fp'], st['u'], st['par']
        wi = fp % NWB
        wkB = ('wB', wi)
        tok = 256 * u
        Gb = par
        Ub = 2 + par
        uk = ('u2T', u)
        MMS([(PS(Gb)[:, 256 * j:256 * (j + 1)], [(wgB[wi][:, k, 128 * j:128 * (j + 1)], u2T[:, k, tok:tok + 256]) for k in range(8)])
             for j in range(2)], [wkB, uk], [pk(Gb)])
        MMS([(PS(Ub)[:, 256 * j:256 * (j + 1)], [(wuB[wi][:, k, 128 * j:128 * (j + 1)], u2T[:, k, tok:tok + 256]) for k in range(8)])
             for j in range(2)], [wkB, uk], [pk(Ub)])

    def ffn_ew(st):
        fp, u, par = st['fp'], st['u'], st['par']
        Gb = par
        Ub = 2 + par
        ats = []
        for j in range(2):
            f = 2 * fp + j
            gi_ = gcnt[0] % NG_
            gcnt[0] += 1
            gk = ('Gs', gi_ % 2)
            CP('scalar', Gs[gi_][:, 0:2], carry[:, f, :], ['carry'], [gk])
            CP('scalar', Gs[gi_][:, 2:258], PS(Gb)[:, 256 * j:256 * (j + 1)], [pk(Gb)], [gk])
            ck_ = ('c1', gi_ % 2)
            ACT(c1[gi_], Gs[gi_][:, 0:256], AF.Copy, [gk, 'cw'], [ck_], scale=cw[:, f, 0:1])
            STT('vector', c1[gi_], Gs[gi_][:, 1:257], cw[:, f, 1:2], c1[gi_], ALU.mult, ALU.add, [gk, 'cw', ck_], [ck_])
            STT('vector', c1[gi_], Gs[gi_][:, 2:258], cw[:, f, 2:3], c1[gi_], ALU.mult, ALU.add, [gk, 'cw', ck_], [ck_])
            CP('scalar', carry[:, f, :], Gs[gi_][:, 256:258], [gk], ['carry'])
            ACT(gl_[gi_], c1[gi_], AF.Gelu_apprx_tanh, [ck_, 'cb'], [('gl', gi_ % 2)], bias=cb[:, f:f + 1])
            TT('vector', AT[gi_], gl_[gi_], PS(Ub)[:, 256 * j:256 * (j + 1)], ALU.mult, [('gl', gi_ % 2), pk(Ub)], [('AT', gi_)])
            ats.append(gi_)
        return ats

    def ffn_down(st, ats):
        fp = st['fp']
        wi = fp % NWB
        wkB = ('wB', wi)
        steps = []
        for t2 in range(2):
            for half in range(2):
                for j in range(2):
                    steps.append((PS(4 + 2 * t2 + half), AT[ats[j]][:, 128 * t2:128 * (t2 + 1)],
                                  wdB[wi][:, j, 512 * half:512 * (half + 1)], st['first'] and j == 0, st['last'] and j == 1))
        MMRAW(steps, [('AT', ats[0]), ('AT', ats[1]), wkB], [pk(4), pk(5), pk(6), pk(7)])

    def ffn_evac(st):
        u = st['u']
        for t2 in range(2):
            hk = ('h', 1 + 2 * u + t2)
            for half in range(2):
                hsl = h_all[:, 2 * u + t2, 512 * half:512 * (half + 1)]
                TT('vector', hsl, hsl, PS(4 + 2 * t2 + half), ALU.add, [pk(4 + 2 * t2 + half), hk], [hk])

    def ffn_halo(fp):
        wi = fp % NWB
        wkB = ('wB', wi)
        MMS([(PS(4)[:, 2 * j:2 * j + 2], [(wgB[wi][:, k, 128 * j:128 * (j + 1)], u2h[:, k, :]) for k in range(8)])
             for j in range(2)], [wkB, 'u2h'], [pk(4)])
        CP('vector', carry[:, 2 * fp:2 * fp + 2, :], PS(4)[:, 0:4].rearrange("p (j t) -> p j t", j=2), [pk(4)], ['carry'])

    NFP = NF // 2
    quads = [tuple(range(a, min(a + 2, NFP))) for a in range(0, NFP, 2)]
    seq = []
    for qi, q in enumerate(quads):
        for u in range(8):
            for idx, fp in enumerate(q):
                seq.append({'fp': fp, 'u': u, 'first': idx == 0, 'last': idx == len(q) - 1, 'qi': qi, 'par': len(seq) % 2})
    NS = len(seq)
    for fp in quads[0]:
        ffn_load(fp)
    for fp in quads[0]:
        ffn_halo(fp)
    ffn_gu(seq[0])
    ffn_gu(seq[1])
    ats_q = {0: ffn_ew(seq[0])}
    for si in range(NS):
        st = seq[si]
        qi = st['qi']
        if st['u'] == 0 and st['first'] and qi + 1 < len(quads):
            for fp in quads[qi + 1]:
                ffn_load(fp)
        ffn_down(st, ats_q.pop(si))
        if si + 2 < NS:
            ffn_gu(seq[si + 2])
        if si + 1 < NS:
            ats_q[si + 1] = ffn_ew(seq[si + 1])
        if st['last']:
            ffn_evac(st)
            if st['u'] == 5 and qi + 1 < len(quads):
                for fp in quads[qi + 1]:
                    ffn_halo(fp)
    S.barrier()
    A.release(a_mark)
    gfin = A.alloc([D], F32)
    S.dma('sync', gfin, g_fin, writes=['gfin'])
    pwg = A.alloc([8, D], BF16)
    pwp = A.alloc([2, D], BF16)
    wload(pwg, ple_wg.rearrange("(k p) n -> p k n", p=128), 'pwg')
    wload(pwp, ple_wp.rearrange("(k p) n -> p k n", p=128), 'pwp')
    NB3 = 4
    nscr3 = [alloc_nscr('n3_%d' % i, share_sq=(i > 0)) for i in range(NB3)]
    u3T = [A.alloc([8, 128], BF16) for _ in range(NB3)]
    ptl = [A.alloc([256], F32) for _ in range(NB3)]
    ptb = [A.alloc([256], BF16) for _ in range(NB3)]
    pT = [A.alloc([2, 128], BF16) for _ in range(NB3)]
    sgp = [A.alloc([D], F32) for _ in range(NB3)]; tpl = [A.alloc([D], F32) for _ in range(NB3)]
    st3 = [A.alloc([4], F32) for _ in range(NB3)]; junk = A.alloc([D], BF16)
    otl = [A.alloc([D], F32) for _ in range(NB3)]
    def b3_a(tl):
        bi = tl % NB3
        pbi = tl % 2
        orow = tl * 128
        hk = ('h', tl)
        ht = h_all[:, tl, :]
        norm_T(ht, hk, u3T[bi], ('u3T', bi), gple, 'gple', nscr3[bi], 4 + pbi)
        S.dma('sync', ptl[bi], pe[orow:orow + 128, :], writes=[('ptl', bi)])
        CP('scalar', ptb[bi], ptl[bi], [('ptl', bi)], [('ptb', bi)])
        TRS([(PSB(6 + pbi)[:, kk * 128:(kk + 1) * 128], ptb[bi][:, kk * 128:(kk + 1) * 128], 128) for kk in range(2)],
            [('ptb', bi)], [pk(6 + pbi)])
        CP('scalar', pT[bi].rearrange("p k n -> p (k n)"), PSB(6 + pbi)[:, 0:256], [pk(6 + pbi)], [('pT', bi)])

    def b3_b(tl):
        bi = tl % NB3
        pbi = tl % 2
        orow = tl * 128
        hk = ('h', tl)
        ht = h_all[:, tl, :]
        for half in range(2):
            hs_ = slice(half * 512, (half + 1) * 512)
            MMS([(PS(half), [(u3T[bi][:, k, :], pwg[:, k, hs_]) for k in range(8)])], [('u3T', bi), 'pwg'], [pk(half)])
            ACT(sgp[bi][:, hs_], PS(half), AF.Sigmoid, [pk(half)], [('sgp', bi)])
            MMS([(PS(2 + half), [(pT[bi][:, k, :], pwp[:, k, hs_]) for k in range(2)])], [('pT', bi), 'pwp'], [pk(2 + half)])
            TT('vector', tpl[bi][:, hs_], PS(2 + half), sgp[bi][:, hs_], ALU.mult, [pk(2 + half), ('sgp', bi)], [('tpl', bi)])
        TT('vector', ht, ht, tpl[bi], ALU.add, [('tpl', bi), hk], [hk])

    def b3_c(tl):
        bi = tl % NB3
        orow = tl * 128
        hk = ('h', tl)
        ht = h_all[:, tl, :]
        sk_ = ('st3', bi)
        ACT(junk, ht, AF.Square, [hk], [sk_], accum=st3[bi][:, 0:1])
        ACT(st3[bi][:, 2:3], st3[bi][:, 0:1], AF.Ln, [sk_, 'eps_c'], [sk_], bias=eps_c[:, 0:1], scale=1.0 / D)
        ACT(st3[bi][:, 3:4], st3[bi][:, 2:3], AF.Exp, [sk_], [sk_], scale=-0.5)
        ok_ = ('otl', bi)
        STT('vector', otl[bi], ht, st3[bi][:, 3:4], gfin, ALU.mult, ALU.mult, [hk, sk_, 'gfin'], [ok_])
        S.dma('sync', out_d[orow:orow + 128, :], otl[bi], reads=[ok_])

    b3_a(0)
    b3_a(1)
    for tl in range(16):
        if tl + 2 < 16:
            b3_a(tl + 2)
        b3_b(tl)
        if tl >= 2:
            b3_c(tl - 2)
    b3_c(14)
    b3_c(15)
    return finish(nc, es, S)


def finish(nc, es, S):
    S.barrier()
    zf = getattr(S, 'zero_psum', None)
    if zf is not None:
        zf()
        S.barrier()
    block = es.enter_context(nc.Block())
    S.emit(block)
    es.close()
    return nc


def host_prep(inputs):
    f32 = np.float32
    x = np.asarray(inputs['x'], f32)
    p = np.asarray(inputs['p'], f32)[0]
    sq = lambda k: np.asarray(inputs[k], f32)[0]
    fm = lambda v, k: np.ascontiguousarray(v.reshape(k, 128).T)
    common = {}
    common['ident'] = np.eye(128, dtype=f32)
    bb, aa = np.meshgrid(np.arange(128), np.arange(128), indexing='ij')
    cm = np.zeros((128, 3, 256), f32)
    cm[:, 0, :128] = (bb >= aa)
    cm[:, 1, :128] = (bb <= aa)
    s_idx = np.arange(128) // 16
    cm[:, 2, :128] = (s_idx[None, :] >= s_idx[:, None])
    cm[:, 2, 128:] = 1.0
    common['cmask'] = cm
    pm = np.zeros((128, 128), f32)
    for m_ in range(128):
        d_ = m_ % 64
        src = m_ + 8 if d_ < 8 else (m_ - 8 if d_ < 16 else m_)
        pm[src, m_] = 1.0
    common['permT'] = pm
    sh = np.zeros((128, 64), f32)
    sh[64 + np.arange(64), np.arange(64)] = 1.0
    common['shiftD'] = sh
    common['kp'] = np.broadcast_to(np.array(KP, f32)[None, :], (64, NKP)).copy()
    common['w_in'] = sq('w_in')
    common['g_mix'] = fm(sq('mix_norm_g'), 8)
    common['gate_b'] = fm(sq('gate_b'), 16)
    common['lam_re'] = np.ascontiguousarray(sq('ssm_lam_re').T)
    common['lam_im'] = np.ascontiguousarray(sq('ssm_lam_im').T)
    common['log_dt'] = np.broadcast_to(sq('ssm_log_dt')[None, :], (64, 32)).copy()
    common['b_re'] = np.ascontiguousarray(sq('ssm_b_re').transpose(1, 0, 2))
    common['b_im'] = np.ascontiguousarray(sq('ssm_b_im').transpose(1, 0, 2))
    common['c_re'] = np.ascontiguousarray(sq('ssm_c_re').transpose(2, 0, 1))
    common['c_im'] = np.ascontiguousarray(sq('ssm_c_im').transpose(2, 0, 1))
    common['d_col'] = np.ascontiguousarray(np.tile(sq('ssm_d').T, (8, 1)))
    common['glu_w'] = sq('ssm_glu_w')
    common['glu_b'] = fm(sq('ssm_glu_b'), 4)
    common['w_ba'] = sq('w_branch_a')
    common['w_bb'] = sq('w_branch_b')
    common['w_out'] = sq('w_out')
    common['g_ffn'] = fm(sq('ffn_norm_g'), 8)
    common['w_gate'] = sq('ffn_w_gate')
    common['w_up'] = sq('ffn_w_up')
    common['conv_w'] = np.ascontiguousarray(sq('ffn_conv_w').reshape(3, NF, 128).transpose(2, 1, 0))
    common['conv_b'] = fm(sq('ffn_conv_b'), NF)
    common['w_down'] = sq('ffn_w_down')
    common['g_ple'] = fm(sq('ple_norm_g'), 8)
    common['ple_wg'] = sq('ple_w_gate')
    common['ple_wp'] = sq('ple_w_proj')
    common['g_fin'] = np.broadcast_to(np.asarray(inputs['final_norm_g'], f32)[None, :], (128, D)).copy()
    half = 8
    freqs = (np.float32(500000.0) ** (-np.arange(half, dtype=f32) * np.float32(2.0 / 16))).astype(f32)
    maps = []
    for core in range(8):
        b, q = core // 4, core % 4
        s = q * OWN
        t0 = s - (EXT - OWN)
        m = dict(common)
        xe_ = np.zeros((EXT, D), f32)
        lo = max(t0, 0)
        xe_[lo - t0:] = x[b, lo:s + OWN]
        m['xe'] = xe_
        m['pe'] = np.ascontiguousarray(p[b, s:s + OWN])
        tokn = (s - 128 - 2048) + np.arange(OWN17 + 2048)
        posn = np.maximum(tokn, 0).astype(f32)
        angn = posn[:, None] * freqs[None, :]
        cn = np.cos(angn).astype(f32).T
        sn_ = np.sin(angn).astype(f32).T
        rotn = np.zeros((96, 2, OWN17 + 2048), f32)
        rotn[:, 0] = 1.0
        for hb in (0, 64):
            rotn[hb:hb + 8, 0] = cn; rotn[hb + 8:hb + 16, 0] = cn
            rotn[hb:hb + 8, 1] = -sn_; rotn[hb + 8:hb + 16, 1] = sn_
        m['rotn'] = rotn
        for g, dil in enumerate(DILS):
            nk = OWN17 + 128 * dil
            base = s - 128 - 128 * dil
            per = nk // dil
            r_ = np.arange(nk) // per
            i_ = np.arange(nk) % per
            tok = base + r_ + dil * i_
            v = (tok >= 0).astype(f32)
            nblk = (per + 127) // 128
            vv = np.zeros((128, 64), f32)
            for r in range(dil):
                for j in range(nblk):
                    seg = v[r * per + j * 128: min(r * per + (j + 1) * 128, (r + 1) * per)]
                    vv[:len(seg), j * dil + r] = seg
            m['valid%d' % g] = vv
        maps.append(m)
    return maps


_CACHE = {}


def kernel(**inputs):
    maps = host_prep(inputs)
    if 'nc' not in _CACHE:
        nc = bass.Bass("TRN2", target_bir_lowering=False)
        build_full(nc)
        _CACHE['nc'] = nc
    nc = _CACHE['nc']
    res = run_bass_kernel_spmd(nc, maps, core_ids=list(range(8)))
    out = np.zeros((2, SEQ, D), np.float32)
    for core in range(8):
        b, q = core // 4, core % 4
        out[b, q * OWN:(q + 1) * OWN] = res.results[core]["out"]
    return out
```
